# Optimizing a Trainium2 kernel written in Bass

```python
import jax, jax.numpy as jnp
from jax import lax
import numpy as np

D_MODEL = 2048
BATCH = 4
SEQ = 4096
DEPTH = 1

HEAD_DIM = 128
N_ATT_HEADS = D_MODEL // (2 * HEAD_DIM)
N_RET_HEADS = D_MODEL // (2 * HEAD_DIM)
D_ATT = N_ATT_HEADS * HEAD_DIM
D_RET = N_RET_HEADS * HEAD_DIM
D_MIX = D_ATT + D_RET
D_IN_PROJ = 3 * D_ATT + 4 * D_RET
DILATED_PATTERNS = ((128, 1), (512, 4), (2048, 16))
RET_CHUNK = 128
D_FF = -(-(8 * D_MODEL) // (3 * 256)) * 256
EPS = 1e-6

kernel_name = 'hybrid_dilated_attn_retention_block'


def rmsnorm(x, w):
    xf = x.astype(jnp.float32)
    xf = xf * lax.rsqrt(jnp.mean(xf * xf, axis=-1, keepdims=True) + EPS)
    return (xf * w.astype(jnp.float32)).astype(x.dtype)


def alibi_slopes(n_heads):
    return jnp.exp2(-8.0 * jnp.arange(1, n_heads + 1, dtype=jnp.float32) / n_heads)


def dilated_window_partial(q, k, v, slopes, window, dilation):
    B, H, S, Dh = q.shape
    half = window // (2 * dilation)
    blk = half
    L = S // dilation
    nb = -(-L // blk)
    Lp = nb * blk

    def to_sub(t):
        return t.reshape(B, H, L, dilation, Dh).transpose(0, 1, 3, 2, 4)

    qb = jnp.pad(to_sub(q), ((0, 0), (0, 0), (0, 0), (0, Lp - L), (0, 0)))
    qb = qb.reshape(B, H, dilation, nb, blk, Dh)

    def key_windows(t):
        tp = jnp.pad(to_sub(t), ((0, 0), (0, 0), (0, 0), (blk, Lp - L + blk), (0, 0)))
        tb = tp.reshape(B, H, dilation, nb + 2, blk, Dh)
        return jnp.concatenate([tb[:, :, :, :-2], tb[:, :, :, 1:-1], tb[:, :, :, 2:]], axis=4)

    kw = key_windows(k)
    vw = key_windows(v)
    s = jnp.einsum('bhrnqd,bhrnkd->bhrnqk', qb, kw) * (Dh ** -0.5)
    lq = jnp.arange(nb)[:, None] * blk + jnp.arange(blk)[None, :]
    lk = jnp.arange(nb)[:, None] * blk - blk + jnp.arange(3 * blk)[None, :]
    dist = jnp.abs(lq[:, :, None] - lk[:, None, :])
    valid = (dist <= half) & (lk[:, None, :] >= 0) & (lk[:, None, :] < L)
    bias = -slopes[:, None, None, None, None] * (dilation * dist).astype(jnp.float32)
    s = jnp.where(valid, s + bias, -jnp.inf)
    m = jnp.max(s, axis=-1)
    p = jnp.exp(s - m[..., None])
    den = jnp.sum(p, axis=-1)
    num = jnp.einsum('bhrnqk,bhrnkd->bhrnqd', p, vw)
    num = num.reshape(B, H, dilation, Lp, Dh)[:, :, :, :L].transpose(0, 1, 3, 2, 4).reshape(B, H, S, Dh)
    m = m.reshape(B, H, dilation, Lp)[..., :L].transpose(0, 1, 3, 2).reshape(B, H, S)
    den = den.reshape(B, H, dilation, Lp)[..., :L].transpose(0, 1, 3, 2).reshape(B, H, S)
    return num, m, den


def dilated_mixture_attention(q, k, v, slopes):
    q, k, v = (t.astype(jnp.float32) for t in (q, k, v))
    parts = [dilated_window_partial(q, k, v, slopes, w, d) for (w, d) in DILATED_PATTERNS]
    m_all = jnp.max(jnp.stack([p[1] for p in parts], axis=0), axis=0)
    num = 0.0
    den = 0.0
    for (n_i, m_i, d_i) in parts:
        w_i = jnp.exp(m_i - m_all)
        num = num + w_i[..., None] * n_i
        den = den + w_i * d_i
    return num / den[..., None]


def retention_direction(q, k, v, log_gamma, strict):
    B, H, S, Dh = q.shape
    C = RET_CHUNK
    nc = S // C
    idx = jnp.arange(C, dtype=jnp.float32)
    rel = idx[:, None] - idx[None, :]
    inside = (rel > 0) if strict else (rel >= 0)
    decay_mask = jnp.where(inside, jnp.exp(log_gamma[:, None, None] * jnp.maximum(rel, 0.0)), 0.0)
    q_dec = jnp.exp(log_gamma[:, None] * (idx + 1.0))[..., None]
    k_dec = jnp.exp(log_gamma[:, None] * (C - 1.0 - idx))[..., None]
    chunk_dec = jnp.exp(log_gamma * C)[:, None, None]

    def to_chunks(t):
        return t.reshape(B, H, nc, C, Dh).transpose(2, 0, 1, 3, 4)

    def step(state, qkv):
        qc, kc, vc = qkv
        inner = jnp.einsum('bhid,bhjd->bhij', qc, kc) * decay_mask
        o = jnp.einsum('bhij,bhjd->bhid', inner, vc) + jnp.einsum('bhid,bhde->bhie', qc * q_dec, state)
        state = state * chunk_dec + jnp.einsum('bhjd,bhje->bhde', kc * k_dec, vc)
        return state, o

    state0 = jnp.zeros((B, H, Dh, Dh), jnp.float32)
    _, o = lax.scan(step, state0, (to_chunks(q), to_chunks(k), to_chunks(v)))
    return o.transpose(1, 2, 0, 3, 4).reshape(B, H, S, Dh)


def bidirectional_retention(q, k, v, decay_fwd, decay_bwd):
    q, k, v = (t.astype(jnp.float32) for t in (q, k, v))
    q = q * (q.shape[-1] ** -0.5)
    lg_f = -jnp.exp(decay_fwd.astype(jnp.float32))
    lg_b = -jnp.exp(decay_bwd.astype(jnp.float32))
    o_f = retention_direction(q, k, v, lg_f, strict=False)
    flip = lambda t: jnp.flip(t, axis=2)
    o_b = flip(retention_direction(flip(q), flip(k), flip(v), lg_b, strict=True))
    return o_f + o_b


def setup_inputs(seed: int = 0) -> dict:
    key = jax.random.key(seed)
    ks = jax.random.split(key, 16)
    f32 = jnp.float32
    nrm = lambda k, shape, scale: jax.random.normal(k, shape, f32) * scale
    gain = lambda k, shape: 1.0 + 0.01 * jax.random.normal(k, shape, f32)
    base = np.log(-np.log(1.0 - 2.0 ** (-5.0 - np.arange(N_RET_HEADS)))).astype(np.float32)
    base = jnp.asarray(base)[None, :]
    return {
        'x': jax.random.normal(ks[0], (BATCH, SEQ, D_MODEL), f32),
        'norm_mix_w': gain(ks[1], (DEPTH, D_MODEL)),
        'w_in': nrm(ks[2], (DEPTH, D_MODEL, D_IN_PROJ), D_MODEL ** -0.5),
        'ret_decay_fwd': base + 0.05 * jax.random.normal(ks[3], (DEPTH, N_RET_HEADS), f32),
        'ret_decay_bwd': base + 0.05 * jax.random.normal(ks[4], (DEPTH, N_RET_HEADS), f32),
        'ret_norm_w': gain(ks[5], (DEPTH, D_RET)),
        'w_out': nrm(ks[6], (DEPTH, D_MIX, D_MODEL), D_MIX ** -0.5),
        'norm_ffn_w': gain(ks[7], (DEPTH, D_MODEL)),
        'w_gate': nrm(ks[8], (DEPTH, D_MODEL, D_FF), D_MODEL ** -0.5),
        'w_up': nrm(ks[9], (DEPTH, D_MODEL, D_FF), D_MODEL ** -0.5),
        'w_down': nrm(ks[10], (DEPTH, D_FF, D_MODEL), D_FF ** -0.5),
        'norm_final_w': gain(ks[11], (D_MODEL,)),
    }


def reference(x, norm_mix_w, w_in, ret_decay_fwd, ret_decay_bwd, ret_norm_w, w_out,
              norm_ffn_w, w_gate, w_up, w_down, norm_final_w):
    B, S, _ = x.shape
    slopes = alibi_slopes(N_ATT_HEADS)
    splits = [D_ATT, 2 * D_ATT, 3 * D_ATT, 3 * D_ATT + D_RET, 3 * D_ATT + 2 * D_RET, 3 * D_ATT + 3 * D_RET]

    def heads(t):
        return t.reshape(B, S, -1, HEAD_DIM).transpose(0, 2, 1, 3)

    def merge(t):
        return t.transpose(0, 2, 1, 3).reshape(B, S, -1)

    h = x
    for layer in range(DEPTH):
        n = rmsnorm(h, norm_mix_w[layer])
        proj = n @ w_in[layer]
        q_a, k_a, v_a, q_r, k_r, v_r, g_r = jnp.split(proj, splits, axis=-1)
        attn = dilated_mixture_attention(heads(q_a), heads(k_a), heads(v_a), slopes)
        ret = bidirectional_retention(heads(q_r), heads(k_r), heads(v_r),
                                      ret_decay_fwd[layer], ret_decay_bwd[layer])
        ret = ret * lax.rsqrt(jnp.mean(ret * ret, axis=-1, keepdims=True) + EPS)
        ret = merge(ret) * ret_norm_w[layer].astype(jnp.float32)
        ret = ret * jax.nn.silu(g_r.astype(jnp.float32))
        mixed = jnp.concatenate([merge(attn), ret], axis=-1).astype(x.dtype)
        h = h + mixed @ w_out[layer]
        n2 = rmsnorm(h, norm_ffn_w[layer])
        h = h + (jax.nn.silu(n2 @ w_gate[layer]) * (n2 @ w_up[layer])) @ w_down[layer]
    return rmsnorm(h, norm_final_w)
```

```python
import numpy as np
import ml_dtypes
import concourse.bass as bass
import concourse.mybir as mybir
from concourse.bass_utils import run_bass_kernel_spmd

F32 = mybir.dt.float32
BF16 = mybir.dt.bfloat16
AF = mybir.ActivationFunctionType
ALU = mybir.AluOpType
AX = mybir.AxisListType

ENGS = ("pe", "act", "dve", "pool", "sp")


class Buf:
    def __init__(self, name):
        self.name = name
        self.w = None
        self.r = {}


class Sched:
    def __init__(self, nc):
        self.nc = nc
        self.ops = {e: [] for e in ENGS}
        self.cnt = {e: 0 for e in ENGS}
        self.known = {e: {} for e in ENGS}
        self.dma_cnt = {}
        self.n_inst = {e: 0 for e in ENGS}

    def _deps(self, eng, reads, writes):
        deps = []
        for b in reads:
            if b.w is not None:
                deps.append(b.w)
        for b in writes:
            if b.w is not None:
                deps.append(b.w)
            deps.extend(b.r.items())
        need = {}
        for key, val in deps:
            if key == "pe" and eng == "pe":
                continue
            if self.known[eng].get(key, 0) >= val:
                continue
            if need.get(key, 0) < val:
                need[key] = val
        for key, val in need.items():
            self.known[eng][key] = val
            self.ops[eng].append(("wait", key, val))

    def _record(self, ev, reads, writes):
        key, val = ev
        for b in reads:
            if b.r.get(key, 0) < val:
                b.r[key] = val
        for b in writes:
            b.w = ev
            b.r = {}

    def op(self, eng, fn, reads=(), writes=(), signal=True):
        self._deps(eng, reads, writes)
        if signal:
            self.cnt[eng] += 1
            ev = (eng, self.cnt[eng])
        else:
            ev = (eng, self.cnt[eng] + 1)
        self.ops[eng].append(("op", fn, signal))
        self.n_inst[eng] += 1
        self._record(ev, reads, writes)

    def dma(self, eng, out_ap, in_ap, key, reads=(), writes=()):
        self._deps(eng, reads, writes)
        k = ("dma", key)
        self.dma_cnt[k] = self.dma_cnt.get(k, 0) + 16
        ev = (k, self.dma_cnt[k])
        self.ops[eng].append(("dma", out_ap, in_ap, k))
        self.n_inst[eng] += 1
        self._record(ev, reads, writes)

    def prewait(self, eng, bufs):
        self._deps(eng, [], bufs)

    def wait_all(self, eng, bufs):
        self._deps(eng, [], bufs)

    def emit(self):
        nc = self.nc
        from contextlib import ExitStack
        with ExitStack() as st:
            sems = {}
            for e in ("pe", "act", "dve", "pool"):
                sems[e] = st.enter_context(nc.semaphore("s_" + e))
            for i, k in enumerate(sorted(self.dma_cnt.keys(), key=str)):
                sems[k] = st.enter_context(nc.semaphore("d%d" % i))
            block = st.enter_context(nc.Block())

            def run(eng_name, eng):
                for o in self.ops[eng_name]:
                    if o[0] == "wait":
                        eng.wait_ge(sems[o[1]], o[2])
                    elif o[0] == "op":
                        ins = o[1](eng)
                        if o[2]:
                            ins.then_inc(sems[eng_name], 1)
                    else:
                        eng.dma_start(out=o[1], in_=o[2]).then_inc(sems[o[3]], 16)

            @block.tensor
            def _(eng):
                run("pe", eng)

            @block.scalar
            def _(eng):
                run("act", eng)

            @block.vector
            def _(eng):
                run("dve", eng)

            @block.gpsimd
            def _(eng):
                run("pool", eng)

            @block.sync
            def _(eng):
                run("sp", eng)


class Region:
    def __init__(self, arena, off, size, name, prior, nbytes):
        self.arena, self.off, self.size, self.name = arena, off, size, name
        self.n = nbytes // 2
        self.prior = prior
        self.bufs = []

    def buf(self, name=None):
        b = Buf(name or self.name)
        b.r = dict(self.prior)
        self.bufs.append(b)
        return b

    def bf(self):
        return self.arena.t[:, self.off:self.off + self.n]

    def f32(self):
        return self.arena.t[:, self.off:self.off + self.n].bitcast(F32)


class Arena:
    def __init__(self, tensor, total):
        self.t, self.total = tensor, total
        self.live = []
        self.dead = []
        self.peak = 0

    def alloc(self, nbytes, name):
        size = ((nbytes + 63) // 64) * 32
        self.live.sort(key=lambda r: r.off)
        off = 0
        for r in self.live:
            if r.off - off >= size:
                break
            off = max(off, r.off + r.size)
        if off + size > self.total:
            raise RuntimeError("arena OOM for %s (%d B); live=%s" % (
                name, nbytes, [(r.name, r.size * 2) for r in self.live]))
        prior = {}
        for (o, sz, ev) in self.dead:
            if o < off + size and off < o + sz:
                for k, v in ev.items():
                    if prior.get(k, 0) < v:
                        prior[k] = v
        reg = Region(self, off, size, name, prior, nbytes)
        self.live.append(reg)
        self.peak = max(self.peak, off + size)
        return reg

    def free(self, reg):
        self.live.remove(reg)
        ev = dict(reg.prior)
        for b in reg.bufs:
            items = list(b.r.items())
            if b.w is not None:
                items.append(b.w)
            for k, v in items:
                if ev.get(k, 0) < v:
                    ev[k] = v
        self.dead.append((reg.off, reg.size, ev))


D = 2048
T_OWN = 2048
KC = 16
DFF = 5632
NH = 8
EPS = 1e-6
C_ID, C_DPOS, C_DNEG, C_IQ1, C_IQR, C_PL, C_PR, C_ABSD, C_VALID, NCST = 0, 128, 256, 384, 512, 640, 641, 642, 898, 1154
MERGE_DEN = True
NXT = 4
ARENA_ELEMS = 105472


def host_consts():
    c = np.zeros((128, NCST), np.float32)
    p = np.arange(128)[:, None].astype(np.float32)
    i = np.arange(128)[None, :].astype(np.float32)
    c[:, C_ID:C_ID + 128] = np.eye(128, dtype=np.float32)
    c[:, C_DPOS:C_DPOS + 128] = np.maximum(i - p, 0)
    c[:, C_DNEG:C_DNEG + 128] = np.maximum(p - i, 0)
    c[:, C_IQ1:C_IQ1 + 128] = i + 1
    c[:, C_IQR:C_IQR + 128] = 128 - i
    c[:, C_PL] = 127 - p[:, 0]
    c[:, C_PR] = p[:, 0]
    cc = np.arange(256)[None, :].astype(np.float32)
    c[:, C_ABSD:C_ABSD + 256] = np.abs(cc - 64 - p)
    c[:, C_VALID:C_VALID + 256] = ((cc >= p) & (cc <= p + 128)).astype(np.float32)
    return c


def build_program(dbg=()):
    nc = bass.Bass("TRN2", target_bir_lowering=False)
    dt = nc.dram_tensor
    x = dt("x", [4096, D], F32, kind="ExternalInput").ap()
    w_in = dt("w_in", [D, 7168], F32, kind="ExternalInput").ap()
    w_out = dt("w_out", [D, D], F32, kind="ExternalInput").ap()
    w_gate = dt("w_gate", [D, DFF], F32, kind="ExternalInput").ap()
    w_up = dt("w_up", [D, DFF], F32, kind="ExternalInput").ap()
    w_down = dt("w_down", [DFF, D], F32, kind="ExternalInput").ap()
    cst_d = dt("cst", [128, NCST], F32, kind="ExternalInput").ap()
    pp_d = dt("pp", [128, 40], F32, kind="ExternalInput").ap()
    bcv_d = dt("bcv", [D + 16], F32, kind="ExternalInput").ap()
    idxm_d = dt("idxm", [2048], F32, kind="ExternalInput").ap()
    y = dt("y", [T_OWN, D], F32, kind="ExternalOutput").ap()
    mix_d = dt("mixT", [16, 128, T_OWN], BF16, kind=("ExternalOutput" if "mixT" in dbg else "Internal")).ap()
    dbg_out = {}

    S = Sched(nc)
    arena_t = nc.alloc_sbuf_tensor("arena", [128, ARENA_ELEMS], BF16)
    AR = Arena(arena_t, ARENA_ELEMS)
    banks = [nc.alloc_psum_tensor("bank%d" % i, [128, 512], F32) for i in range(8)]
    PB = [Buf("bank%d" % i) for i in range(8)]
    bank_rr = {"n": 0}

    def psum(which=None):
        if which is None:
            which = range(8)
        which = list(which)
        i = which[bank_rr["n"] % len(which)]
        bank_rr["n"] += 1
        return i, banks[i][:], PB[i]

    def MM(out, lhsT, rhs, start, stop, reads, writes, signal, skip=False):
        S.op("pe", lambda e: e.matmul(out, lhsT=lhsT, rhs=rhs, start=start, stop=stop, skip_group_check=skip),
             reads, writes, signal)

    def TR(out, in_, ident, reads, writes, signal):
        S.op("pe", lambda e: e.transpose(out, in_, ident), reads, writes, signal)

    def ACT(out, in_, func, reads, writes, scale=1.0, bias=None, accum=None):
        kw = {}
        if bias is not None:
            kw["bias"] = bias
        if accum is not None:
            kw["accum_out"] = accum
        S.op("act", lambda e: e.activation(out, in_, func, scale=scale, **kw), reads, writes)

    def CP(eng, out, in_, reads, writes):
        if eng == "act":
            S.op("act", lambda e: e.activation(out, in_, AF.Copy), reads, writes)
        else:
            S.op(eng, lambda e: e.tensor_copy(out, in_), reads, writes)

    def TT(eng, out, a, b, op, reads, writes):
        S.op(eng, lambda e: e.tensor_tensor(out, a, b, op=op), reads, writes)

    def STT(eng, out, in0, scalar, in1, op0, op1, reads, writes):
        S.op(eng, lambda e: e.scalar_tensor_tensor(out, in0, scalar, in1, op0=op0, op1=op1), reads, writes)

    def TS(eng, out, in0, s1, s2, op0, op1, reads, writes):
        if s2 is None:
            S.op(eng, lambda e: e.tensor_scalar(out, in0, s1, None, op0=op0), reads, writes)
        else:
            S.op(eng, lambda e: e.tensor_scalar(out, in0, s1, s2, op0=op0, op1=op1), reads, writes)

    def MS(eng, out, val, writes):
        S.op(eng, lambda e: e.memset(out, val), (), writes)

    def RECIP(out, in_, reads, writes):
        S.op("dve", lambda e: e.reciprocal(out, in_), reads, writes)

    rr = {"n": 0}

    def alt():
        rr["n"] += 1
        return "act" if rr["n"] % 2 else "dve"

    R_idf = AR.alloc(128 * 4, "identf"); identf = R_idf.f32(); B_idf = R_idf.buf()
    R_idb = AR.alloc(128 * 2, "identb"); identb = R_idb.bf(); B_idb = R_idb.buf()
    R_ones = AR.alloc(128 * 2, "ones"); onesb = R_ones.bf(); B_ones = R_ones.buf()
    R_pp = AR.alloc(40 * 4, "pp"); pp = R_pp.f32(); B_pp = R_pp.buf()
    R_eps = AR.alloc(64, "eps"); epsb = R_eps.f32()[:, 0:1]; oneb = R_eps.f32()[:, 1:2]; B_eps = R_eps.buf()
    R_sm = AR.alloc(64 * 4, "small"); sm = R_sm.f32(); B_sm = R_sm.buf()
    lgL, lgR, kdL, kdR, gCL, gCR = (sm[:, 8 * i:8 * i + 8] for i in range(6))
    R_srb = AR.alloc(NH * 128 * 4, "SRb"); SRb = R_srb.f32().rearrange("p (h e) -> p h e", h=NH); B_srb = [R_srb.buf("SRb%d" % h) for h in range(NH)]

    R_bcv = AR.alloc((D + 16) * 4, "bcv"); bcv = R_bcv.f32(); B_bcv = R_bcv.buf()
    R_E = AR.alloc(24 * 256 * 2, "alibiE"); Emask = R_E.bf().rearrange("p (m c) -> p m c", m=24); B_E = R_E.buf()
    R_halo = AR.alloc(2 * NH * 1024 * 2, "halo"); halo = R_halo.bf().rearrange("p (a h t) -> p a h t", a=2, h=NH)
    B_halo = [[R_halo.buf("halo%d_%d" % (a, h)) for h in range(NH)] for a in range(2)]
    R_nT = AR.alloc(KC * 2048 * 2, "nT"); nT = R_nT.bf().rearrange("p (c t) -> p c t", c=KC)
    B_nT = [R_nT.buf("nT%d" % g) for g in range(4)]

    R_cst = AR.alloc(NCST * 4, "cst"); cst = R_cst.f32(); B_cst = R_cst.buf()
    S.dma("sp", cst, cst_d, "cst", writes=[B_cst])
    S.dma("sp", pp, pp_d, "pp", writes=[B_pp])
    S.dma("sp", bcv, bcv_d.partition_broadcast(128), "bcv", writes=[B_bcv])
    CP("dve", identf, cst[:, C_ID:C_ID + 128], [B_cst], [B_idf])
    CP("dve", identb, cst[:, C_ID:C_ID + 128], [B_cst], [B_idb])
    MS("dve", onesb, 1.0, [B_ones])
    MS("dve", epsb, EPS, [B_eps])
    MS("dve", oneb, 1.0, [B_eps])
    ACT(sm[:, 0:16], bcv[:, D:D + 16], AF.Exp, [B_bcv], [B_sm])
    TS("dve", sm[:, 0:16], sm[:, 0:16], -1.0, None, ALU.mult, None, [B_sm], [B_sm])
    ACT(kdL, lgL, AF.Exp, [B_sm, B_cst], [B_sm], scale=cst[:, C_PL:C_PL + 1])
    ACT(kdR, lgR, AF.Exp, [B_sm, B_cst], [B_sm], scale=cst[:, C_PR:C_PR + 1])
    ACT(sm[:, 32:48], sm[:, 0:16], AF.Exp, [B_sm], [B_sm], scale=128.0)
    R_tmp = AR.alloc(512 * 4, "setup_tmp"); tmpf = R_tmp.f32(); B_tmp = R_tmp.buf()
    for d_i, dil in enumerate((1, 4, 16)):
        for h in range(NH):
            slope = 2.0 ** (-(h + 1))
            ACT(tmpf[:, 0:256], cst[:, C_ABSD:C_ABSD + 256], AF.Exp, [B_cst], [B_tmp], scale=-slope * dil)
            TT("dve", Emask[:, d_i * NH + h, :], tmpf[:, 0:256], cst[:, C_VALID:C_VALID + 256], ALU.mult, [B_tmp, B_cst], [B_E])
    AR.free(R_tmp)
    AR.free(R_cst)

    def norm_tokens(tok0):
        R_xt = [AR.alloc(D * 4, "xt%d" % i) for i in range(NXT)]
        R_nb = [AR.alloc(D * 2, "nb%d" % i) for i in range(3)]
        R_junk = AR.alloc(D * 2, "junk")
        R_sts = [AR.alloc(64, "nstat%d" % i) for i in range(2)]
        xt = [r.f32() for r in R_xt]; Bxt = [r.buf() for r in R_xt]
        nb = [r.bf() for r in R_nb]; Bnb = [r.buf() for r in R_nb]
        junk = R_junk.bf(); Bjunk = R_junk.buf()
        def stage_a(s):
            sl = s % 2
            st = R_sts[sl].f32(); Bst = R_sts[sl].bufs[0] if R_sts[sl].bufs else R_sts[sl].buf()
            xs = s % NXT
            ns_ = s % 3
            S.dma("sp", xt[xs], x[tok0 + s * 128: tok0 + (s + 1) * 128, :], "xt%d" % xs, writes=[Bxt[xs]])
            MS("dve", st[:, 0:1], 0.0, [Bst])
            ACT(junk, xt[xs], AF.Square, [Bxt[xs]], [Bjunk, Bst], scale=float(D ** -0.5), accum=st[:, 0:1])
            ACT(st[:, 1:2], st[:, 0:1], AF.Ln, [Bst, B_eps], [Bst], bias=epsb)
            ACT(st[:, 2:3], st[:, 1:2], AF.Exp, [Bst], [Bst], scale=-0.5)
            STT("dve", nb[ns_], xt[xs], st[:, 2:3], bcv[:, 0:D], ALU.mult, ALU.mult, [Bxt[xs], Bst, B_bcv], [Bnb[ns_]])

        def stage_b(s):
            ns_ = s % 3
            g = s // 4
            for half in range(2):
                bi, bk, bb = psum([6, 7])
                bkb = bk.bitcast(BF16)
                for c in range(8):
                    cc = half * 8 + c
                    TR(bkb[:, c * 128:(c + 1) * 128], nb[ns_][:, cc * 128:(cc + 1) * 128], identb, [Bnb[ns_], B_idb], [bb], c == 7)
                CP("act" if half == 0 else "dve", nT[:, half * 8:half * 8 + 8, s * 128:(s + 1) * 128], bkb.rearrange("p (c t) -> p c t", c=8), [], [bb, B_nT[g]])
        stage_a(0)
        for s in range(16):
            if s + 1 < 16:
                stage_a(s + 1)
            stage_b(s)
        for r in R_xt + R_nb + [R_junk] + R_sts:
            AR.free(r)

    def make_wslots(n, ncols, name):
        regs = [AR.alloc(KC * ncols * 2, "%s%d" % (name, i)) for i in range(n)]
        return {"regs": regs, "aps": [r.bf().rearrange("p (c n) -> p c n", c=KC) for r in regs],
                "bufs": [r.buf() for r in regs], "n": 0, "name": name}

    def wload(ws, w_ap, col0, ncols):
        i = ws["n"] % len(ws["regs"])
        ws["n"] += 1
        src = w_ap.rearrange("(c p) n -> p c n", p=128)[:, :, col0:col0 + ncols]
        S.dma("pool", ws["aps"][i][:, :, 0:ncols], src, "%s%d" % (ws["name"], i), writes=[ws["bufs"][i]])
        return ws["aps"][i], ws["bufs"][i]

    def proj_block(wap, wbuf, wc0, tgs, evac, banks_sel=(0, 1, 2, 3)):
        for tg in tgs:
            bi, bk, bb = psum(banks_sel)
            for c in range(KC):
                MM(bk, wap[:, c, wc0:wc0 + 128], nT[:, c, tg * 512:(tg + 1) * 512], c == 0, c == KC - 1,
                   [wbuf, B_nT[tg]], [bb], c == KC - 1)
            evac(tg, bk, bb)

    norm_tokens(2048)
    W8 = make_wslots(3, 256, "w8_")
    for a, cbase in ((0, 1024), (1, 2048)):
        for u in range(4):
            wap, wbuf = wload(W8, w_in, cbase + u * 256, 256)
            for hh in range(2):
                h = 2 * u + hh

                def ev(tg, bk, bb, a=a, h=h):
                    CP(alt(), halo[:, a, h, tg * 512:(tg + 1) * 512], bk, [], [bb, B_halo[a][h]])
                proj_block(wap, wbuf, hh * 128, (0, 1), ev)

    R_idx = AR.alloc(2048 * 4, "idxm"); idxm = R_idx.f32(); B_idx = R_idx.buf()
    S.dma("sp", idxm, idxm_d.partition_broadcast(128), "idxm", writes=[B_idx])
    R_dec = AR.alloc(2048 * 2, "decfull"); decf = R_dec.bf(); B_dec = R_dec.buf()
    R_kT = AR.alloc(2048 * 2, "pre_kT"); pkT = R_kT.bf(); B_pkT = R_kT.buf()
    R_vT = AR.alloc(2048 * 2, "pre_vT"); pvT = R_vT.bf(); B_pvT = R_vT.buf()
    R_ktm = AR.alloc(2048 * 2, "pre_ktm"); pktm = R_ktm.bf().rearrange("p (c d) -> p c d", c=16); B_pktm = R_ktm.buf()
    R_vtm = AR.alloc(2048 * 2, "pre_vtm"); pvtm = R_vtm.bf().rearrange("p (c d) -> p c d", c=16); B_pvtm = R_vtm.buf()
    for u in range(4):
        wk, wkb = wload(W8, w_in, 4096 + u * 256, 256)
        wv, wvb = wload(W8, w_in, 5120 + u * 256, 256)
        for hh in range(2):
            h = 2 * u + hh
            ACT(decf, idxm, AF.Exp, [B_idx, B_sm], [B_dec], scale=lgR[:, h:h + 1])

            def evk(tg, bk, bb):
                TT("dve", pkT[:, tg * 512:(tg + 1) * 512], bk, decf[:, tg * 512:(tg + 1) * 512], ALU.mult, [B_dec], [bb, B_pkT])

            def evv(tg, bk, bb):
                CP("act", pvT[:, tg * 512:(tg + 1) * 512], bk, [], [bb, B_pvT])
            proj_block(wk, wkb, hh * 128, range(4), evk)
            proj_block(wv, wvb, hh * 128, range(4), evv)
            for (srcT, Bsrc, dst, Bdst) in ((pkT, B_pkT, pktm, B_pktm), (pvT, B_pvT, pvtm, B_pvtm)):
                for half in range(2):
                    bi, bk, bb = psum([6, 7])
                    bkb = bk.bitcast(BF16)
                    for c in range(8):
                        cc = half * 8 + c
                        TR(bkb[:, c * 128:(c + 1) * 128], srcT[:, cc * 128:(cc + 1) * 128], identb, [Bsrc, B_idb], [bb], c == 7)
                    CP(alt(), dst[:, half * 8:half * 8 + 8, :], bkb.rearrange("p (c t) -> p c t", c=8), [], [bb, Bdst])
            bi, bk, bb = psum([4, 5])
            for c in range(16):
                MM(bk[:, 0:128], pktm[:, c, :], pvtm[:, c, :], c == 0, c == 15, [B_pktm, B_pvtm], [bb], c == 15)
            CP("dve", SRb[:, h, :], bk[:, 0:128], [], [bb, B_srb[h]])
    for r in (R_idx, R_dec, R_kT, R_vT, R_ktm, R_vtm):
        AR.free(r)

    for r in W8["regs"]:
        AR.free(r)
    norm_tokens(0)
    W4 = make_wslots(6, 128, "w4_")

    B_mixd = [[Buf("mixd%d_%d" % (i, j)) for j in range(2)] for i in range(16)]

    def spill(h16, src_ap, src_buf, col0, ncols, key):
        wr = [B_mixd[h16][col0 // 1024]] if ncols == 1024 else B_mixd[h16]
        S.dma("sp", mix_d[h16, :, col0:col0 + ncols], src_ap, key, reads=[src_buf], writes=wr)

    AR.free(R_bcv)
    if "skip_attn" not in dbg:
        R_QKV = [[AR.alloc(2048 * 2, "%sT%d" % (nm, i)) for nm in "QKV"] for i in range(2)]
        QKVap = [[r.bf() for r in rr_] for rr_ in R_QKV]
        QKVb = [[r.buf() for r in rr_] for rr_ in R_QKV]
        R_Vt = [AR.alloc(32 * 128 * 2, "Vtm%d" % i) for i in range(2)]
        Vt = [r.bf().rearrange("p (s e) -> p s e", s=32) for r in R_Vt]; B_Vt = [r.buf() for r in R_Vt]
        NP = 4
        R_P = [AR.alloc(2 * 512 * 2, "P%d" % i) for i in range(NP)]
        Pt = [r.bf() for r in R_P]; B_P = [r.buf() for r in R_P]
        R_rc = AR.alloc(1024 * 4, "recip"); rc = R_rc.f32(); B_rc = R_rc.buf()
        R_mo = [AR.alloc(1024 * 2, "mixo%d" % i) for i in range(2)]
        mo = [r.bf() for r in R_mo]; B_mo = [r.buf() for r in R_mo]
        pc = {"n": 0}
        LOOK = 2

        def att_front(st, h):
            QT, KT, VT = QKVap[h % 2]
            B_Q, B_K, B_V = QKVb[h % 2]
            dil, u, n0, n1 = st["dil"], st["u"], st["n0"], st["n1"]
            kp = min(128, st["Lk"] - 128 * u)
            n = n1 - n0
            c0 = n0 - (128 * u - 64)
            G = len(st["rs"])
            bi, bk, bb = psum([2, 3])
            for k_, r_ in enumerate(st["rs"]):
                if u < st["n_own"]:
                    p0 = dil * 128 * u + r_
                    kap = KT[:, p0:p0 + dil * (kp - 1) + 1:dil]; kb_ = B_K
                else:
                    p0 = dil * 128 * u + r_ - 2048
                    kap = halo[:, 0, h, p0:p0 + dil * (kp - 1) + 1:dil]; kb_ = B_halo[0][h]
                q0 = dil * n0 + r_
                qap = QT[:, q0:q0 + dil * (n - 1) + 1:dil]
                MM(bk[0:kp, k_ * n:(k_ + 1) * n], kap, qap, True, True, [kb_, B_Q], [bb], k_ == G - 1)
            ps = pc["n"] % NP
            pc["n"] += 1
            praw = Pt[ps][0:kp, 0:G * n]
            pm = Pt[ps][0:kp, 512:512 + G * n]
            ACT(praw, bk[0:kp, 0:G * n], AF.Exp, [], [bb, B_P[ps]], scale=float(128 ** -0.5))
            em = st["em"]
            if G == 1:
                TT("dve", pm, praw, em[0:kp, c0:c0 + n], ALU.mult, [B_E], [B_P[ps]])
            else:
                TT("dve", pm.rearrange("p (g n) -> p g n", g=G), praw.rearrange("p (g n) -> p g n", g=G),
                   em[0:kp, c0:c0 + n].unsqueeze(1).broadcast_to([kp, G, n]), ALU.mult, [B_E], [B_P[ps]])
            st["ps"], st["kp"], st["n"] = ps, kp, n

        def att_back(st, h):
            dil, u, n0, half, vs = st["dil"], st["u"], st["n0"], st["half"], st["vs"]
            kp, n, ps = st["kp"], st["n"], st["ps"]
            pm = Pt[ps][0:kp, 512:1024]
            mms = []
            G = len(st["rs"])
            r0 = st["rs"][0]
            pm3 = pm[:, 0:G * n].rearrange("p (g n) -> p g n", g=G)
            for part in range(2):
                t00 = dil * n0 - half * 1024
                j0 = max(0, -(-(part * 512 - t00) // dil))
                j1 = min(n - 1, (part * 512 + 511 - t00) // dil)
                if j1 < j0:
                    continue
                cnt = j1 - j0 + 1
                base0 = t00 + dil * j0 - part * 512
                for k_, r_ in enumerate(st["rs"]):
                    tt0 = base0 + r_
                    rhs = pm[:, k_ * n + j0:k_ * n + j0 + cnt]
                    oap = banks[4 + part][:, tt0:tt0 + dil * (cnt - 1) + 1:dil]
                    mms.append((oap, Vt[vs][0:kp, st["slot_of"][(r_, u)], :], rhs, [B_Vt[vs], B_P[ps]], [PB[4 + part]]))
                if MERGE_DEN and dil > 1:
                    dap = banks[6 + part][:, base0:base0 + dil * cnt].rearrange("p (j d) -> p j d", d=dil)[:, :, r0:r0 + G]
                    drhs = pm3[:, :, j0:j0 + cnt].rearrange("p g j -> p j g")
                    mms.append((dap, onesb[0:kp, :], drhs, [B_ones, B_P[ps]], [PB[6 + part]]))
                else:
                    for k_, r_ in enumerate(st["rs"]):
                        tt0 = base0 + r_
                        rhs = pm[:, k_ * n + j0:k_ * n + j0 + cnt]
                        dap = banks[6 + part][:, tt0:tt0 + dil * (cnt - 1) + 1:dil]
                        mms.append((dap, onesb[0:kp, :], rhs, [B_ones, B_P[ps]], [PB[6 + part]]))
            for i_, (o_, l_, r__, rd_, wr_) in enumerate(mms):
                MM(o_, l_, r__, False, False, rd_, wr_, i_ == len(mms) - 1, skip=True)

        def proj_units(h):
            QT, KT, VT = QKVap[h % 2]
            B_Q, B_K, B_V = QKVb[h % 2]
            wq, wqb = wload(W4, w_in, h * 128, 128)
            wk, wkb = wload(W4, w_in, 1024 + h * 128, 128)
            wv, wvb = wload(W4, w_in, 2048 + h * 128, 128)
            out = []

            def mk(wap, wbuf, dst, dbuf, eng, tg):
                stt = {}

                def piece(bs, q4):
                    if q4 == 0:
                        stt["b"] = psum(bs)
                    bi, bk, bb = stt["b"]
                    for c in range(q4 * 4, q4 * 4 + 4):
                        MM(bk, wap[:, c, 0:128], nT[:, c, tg * 512:(tg + 1) * 512], c == 0, c == KC - 1,
                           [wbuf, B_nT[tg]], [bb], c == KC - 1)
                    if q4 == 3:
                        CP(eng, dst[:, tg * 512:(tg + 1) * 512], bk, [], [bb, dbuf])
                for q4 in range(4):
                    out.append(lambda bs, q4=q4: piece(bs, q4))
            for tg in range(4):
                mk(wk, wkb, KT, B_K, "dve", tg)
            for tg in range(4):
                mk(wv, wvb, VT, B_V, "act", tg)
            for tg in range(4):
                mk(wq, wqb, QT, B_Q, "act", tg)
            return out

        for u_ in proj_units(0):
            u_((0, 1, 2, 3))
        for h in range(NH):
            QT, KT, VT = QKVap[h % 2]
            B_Q, B_K, B_V = QKVb[h % 2]
            filler = proj_units(h + 1) if h + 1 < NH else []
            nback = 0
            vslot = 0
            for half in range(2):
                for b_ in (4, 5, 6, 7):
                    MS("dve", banks[b_][:], 0.0, [PB[b_]])
                pending = []
                for d_i, dil in enumerate((1, 4, 16)):
                    em = Emask[:, d_i * NH + h, :]
                    Lq = 1024 // dil
                    qa, qb = half * Lq, (half + 1) * Lq
                    Lk = 2048 // dil + 64
                    n_own_tiles = 2048 // (128 * dil)
                    vs = vslot % 2
                    vslot += 1
                    need = []
                    for r_ in range(dil):
                        for u in range((Lk + 127) // 128):
                            n0, n1 = max(qa, 128 * u - 64), min(qb, 128 * u + 192)
                            if n1 > n0:
                                need.append((r_, u, n0, n1))
                    slot_of = {}
                    tiles = sorted(set((r_, u) for (r_, u, _, _) in need))
                    for gi in range(0, len(tiles), 8):
                        grp = tiles[gi:gi + 8]
                        bi, bk, bb = psum([1])
                        bkb = bk.bitcast(BF16)
                        for k_, (r_, u) in enumerate(grp):
                            kp = min(128, Lk - 128 * u)
                            if u < n_own_tiles:
                                p0 = dil * 128 * u + r_
                                src = VT[:, p0:p0 + dil * (kp - 1) + 1:dil]
                                sb_ = B_V
                            else:
                                p0 = dil * 128 * u + r_ - 2048
                                src = halo[:, 1, h, p0:p0 + dil * (kp - 1) + 1:dil]
                                sb_ = B_halo[1][h]
                            TR(bkb[0:kp, k_ * 128:(k_ + 1) * 128], src, identb, [sb_, B_idb], [bb], k_ == len(grp) - 1)
                            slot_of[(r_, u)] = gi + k_
                        CP("dve", Vt[vs][:, gi:gi + len(grp), :], bkb[:, 0:len(grp) * 128].rearrange("p (s e) -> p s e", s=len(grp)), [], [bb, B_Vt[vs]])
                    groups = {}
                    for (r_, u, n0, n1) in need:
                        groups.setdefault((u, n0, n1), []).append(r_)
                    for (u, n0, n1), rs in sorted(groups.items()):
                        G = max(1, min(4, 512 // (n1 - n0)))
                        for gi in range(0, len(rs), G):
                            st = dict(u=u, n0=n0, n1=n1, rs=rs[gi:gi + G], dil=dil, half=half, vs=vs, slot_of=slot_of,
                                      em=em, Lk=Lk, n_own=n_own_tiles)
                            att_front(st, h)
                            pending.append(st)
                            if len(pending) > LOOK:
                                att_back(pending.pop(0), h)
                                nback += 1
                                if filler:
                                    filler.pop(0)((0,))
                for st in pending:
                    att_back(st, h)
                mslot = (2 * h + half) % 2
                for part in range(2):
                    ACT(rc[:, part * 512:(part + 1) * 512], banks[6 + part][:], AF.Ln, [], [PB[6 + part], B_rc])
                    ACT(rc[:, part * 512:(part + 1) * 512], rc[:, part * 512:(part + 1) * 512], AF.Exp, [], [B_rc], scale=-1.0)
                    TT("dve", mo[mslot][:, part * 512:(part + 1) * 512], banks[4 + part][:], rc[:, part * 512:(part + 1) * 512], ALU.mult, [B_rc], [PB[4 + part], B_mo[mslot]])
                spill(h, mo[mslot], B_mo[mslot], half * 1024, 1024, "mo%d" % mslot)
            for u_ in filler:
                u_((0,))
        for r in [r for rr_ in R_QKV for r in rr_] + [R_rc] + R_Vt + R_P + R_mo:
            AR.free(r)
    AR.free(R_E)
    AR.free(R_halo)

    if "skip_ret" not in dbg:
        R_MT = AR.alloc(NH * 512 * 2, "retMT"); MT = R_MT.bf().rearrange("p (h c) -> p h c", h=NH); B_MT = R_MT.buf()
        R_dq = AR.alloc(2 * NH * 128 * 2, "decq"); decq = R_dq.bf().rearrange("p (a h i) -> p a h i", a=2, h=NH); B_dq = R_dq.buf()
        R_cst = AR.alloc(NCST * 4, "cst2"); cst = R_cst.f32(); B_cst = R_cst.buf()
        S.dma("sp", cst, cst_d, "cst2", writes=[B_cst])
        R_tmp = AR.alloc(512 * 4, "setup_tmp2"); tmpf = R_tmp.f32(); B_tmp = R_tmp.buf()
        for h in range(NH):
            TS("dve", tmpf[:, 0:128], cst[:, C_DPOS:C_DPOS + 128], lgL[:, h:h + 1], None, ALU.mult, None, [B_cst, B_sm], [B_tmp])
            STT("dve", tmpf[:, 128:256], cst[:, C_DNEG:C_DNEG + 128], lgR[:, h:h + 1], tmpf[:, 0:128], ALU.mult, ALU.add, [B_cst, B_sm, B_tmp], [B_tmp])
            for rep in range(4):
                ACT(MT[:, h, rep * 128:(rep + 1) * 128], tmpf[:, 128:256], AF.Exp, [B_tmp], [B_MT])
            ACT(decq[:, 0, h, :], cst[:, C_IQ1:C_IQ1 + 128], AF.Exp, [B_cst, B_sm], [B_dq], scale=lgL[:, h:h + 1])
            ACT(decq[:, 1, h, :], cst[:, C_IQR:C_IQR + 128], AF.Exp, [B_cst, B_sm], [B_dq], scale=lgR[:, h:h + 1])

        AR.free(R_tmp)
        AR.free(R_cst)
        def A2(nbytes, name):
            r = AR.alloc(nbytes, name)
            return r, r.buf()
        RQ = [A2(4096, "r_qT%d" % i) for i in range(2)]
        RK = [A2(4096, "r_kT%d" % i) for i in range(2)]
        RV = [A2(4096, "r_vT%d" % i) for i in range(2)]
        RSG = [A2(4096, "r_sgT%d" % i) for i in range(2)]
        R_e, B_e = A2(512 * 4, "r_e"); re_ = R_e.f32()
        R_ql, B_ql = A2(4096, "r_qdL"); qdL = R_ql.bf()
        R_qr, B_qr = A2(4096, "r_qdR"); qdR = R_qr.bf()
        R_kl, B_kl = A2(4096, "r_kdL"); kdLt = R_kl.bf().rearrange("p (c d) -> p c d", c=16)
        R_kr, B_kr = A2(4096, "r_kdR"); kdRt = R_kr.bf().rearrange("p (c d) -> p c d", c=16)
        R_vt, B_vt = A2(4096, "r_vtm"); vtm = R_vt.bf().rearrange("p (c d) -> p c d", c=16)
        R_U, B_U = A2(2 * 16 * 128 * 4, "r_U"); Ut = R_U.f32().rearrange("p (a c e) -> p a c e", a=2, c=16)
        R_Sb, B_Sb = A2(2 * 16 * 128 * 2, "r_Sbf"); Sbf = R_Sb.bf().rearrange("p (a c e) -> p a c e", a=2, c=16)
        R_cur, B_cur = A2(2 * 128 * 4, "r_cur"); cur = R_cur.f32().rearrange("p (a e) -> p a e", a=2)
        B_curs = [B_cur, R_cur.buf()]
        B_Sbs = [B_Sb, R_Sb.buf()]
        R_pt, B_pt = A2(2 * 512 * 2, "r_PT"); PTt = R_pt.bf().rearrange("p (a c) -> p a c", a=2)
        B_ptl = [B_pt, R_pt.buf()]
        R_ss, B_ss = A2(64 * 4, "r_ss"); ss = R_ss.f32()
        R_on, B_on = A2(4096, "r_on"); on = R_on.bf().rearrange("p (c e) -> p c e", c=16)
        R_junk, B_junk = A2(128 * 2, "r_junk"); rjunk = R_junk.bf()
        R_mr, B_mr = A2(4096, "r_mix"); mr = R_mr.bf()

        def ret_A(h):
            s_ = h % 2
            rqT, B_q = RQ[s_][0].bf(), RQ[s_][1]
            rkT, B_k = RK[s_][0].bf(), RK[s_][1]
            rvT, B_v = RV[s_][0].bf(), RV[s_][1]
            rsg, B_sg = RSG[s_][0].bf(), RSG[s_][1]
            wq, wqb = wload(W4, w_in, 3072 + h * 128, 128)
            wk, wkb = wload(W4, w_in, 4096 + h * 128, 128)
            wv, wvb = wload(W4, w_in, 5120 + h * 128, 128)
            wg, wgb = wload(W4, w_in, 6144 + h * 128, 128)
            out = []

            def mk(wap, wbuf, tg, evac):
                stt = {}

                def piece(bs, q4):
                    if q4 == 0:
                        stt["b"] = psum(bs)
                    bi, bk, bb = stt["b"]
                    for c in range(q4 * 4, q4 * 4 + 4):
                        MM(bk, wap[:, c, 0:128], nT[:, c, tg * 512:(tg + 1) * 512], c == 0, c == KC - 1,
                           [wbuf, B_nT[tg]], [bb], c == KC - 1)
                    if q4 == 3:
                        evac(tg, bk, bb)
                for q4 in range(4):
                    out.append(lambda bs, q4=q4: piece(bs, q4))

            def evg(tg, bk, bb):
                ACT(re_, bk, AF.Exp, [], [bb, B_e], scale=-1.0)
                ACT(re_, re_, AF.Ln, [B_eps], [B_e], bias=oneb)
                ACT(re_, re_, AF.Exp, [], [B_e], scale=-1.0)
                TT("dve", rsg[:, tg * 512:(tg + 1) * 512], bk, re_, ALU.mult, [B_e], [bb, B_sg])
            for tg in range(4):
                mk(wk, wkb, tg, lambda tg_, bk, bb: CP("dve", rkT[:, tg_ * 512:(tg_ + 1) * 512], bk, [], [bb, B_k]))
            for tg in range(4):
                mk(wv, wvb, tg, lambda tg_, bk, bb: CP("act", rvT[:, tg_ * 512:(tg_ + 1) * 512], bk, [], [bb, B_v]))
            for tg in range(4):
                mk(wq, wqb, tg, lambda tg_, bk, bb: ACT(rqT[:, tg_ * 512:(tg_ + 1) * 512], bk, AF.Copy, [], [bb, B_q], scale=float(128 ** -0.5)))
            for tg in range(4):
                mk(wg, wgb, tg, evg)
            return out

        fillq = {"q": []}

        def fill(k):
            for _ in range(k):
                if fillq["q"]:
                    fillq["q"].pop(0)((0, 1))

        def ret_B1(h):
            s_ = h % 2
            rqT, B_q = RQ[s_][0].bf(), RQ[s_][1]
            rkT, B_k = RK[s_][0].bf(), RK[s_][1]
            rvT, B_v = RV[s_][0].bf(), RV[s_][1]
            for half in range(2):
                bi, bk, bb = psum([6, 7])
                bkb = bk.bitcast(BF16)
                for c in range(8):
                    cc = half * 8 + c
                    TR(bkb[:, c * 128:(c + 1) * 128], rkT[:, cc * 128:(cc + 1) * 128], identb, [B_k, B_idb], [bb], c == 7)
                src3 = bkb.rearrange("p (c t) -> p c t", c=8)
                TS("dve", kdLt[:, half * 8:half * 8 + 8, :], src3, kdL[:, h:h + 1], None, ALU.mult, None, [B_sm], [bb, B_kl])
                ACT(kdRt[:, half * 8:half * 8 + 8, :], src3, AF.Copy, [B_sm], [bb, B_kr], scale=kdR[:, h:h + 1])
                fill(2)
                bi, bk, bb = psum([6, 7])
                bkb = bk.bitcast(BF16)
                for c in range(8):
                    cc = half * 8 + c
                    TR(bkb[:, c * 128:(c + 1) * 128], rvT[:, cc * 128:(cc + 1) * 128], identb, [B_v, B_idb], [bb], c == 7)
                CP(alt(), vtm[:, half * 8:half * 8 + 8, :], bkb.rearrange("p (c t) -> p c t", c=8), [], [bb, B_vt])
                fill(2)
            for a, (kt_, kb_) in enumerate(((kdLt, B_kl), (kdRt, B_kr))):
                for g4 in range(4):
                    bi, bk, bb = psum([4, 5])
                    for c4 in range(4):
                        c = g4 * 4 + c4
                        MM(bk[:, c4 * 128:(c4 + 1) * 128], kt_[:, c, :], vtm[:, c, :], True, True, [kb_, B_vt], [bb], c4 == 3)
                    CP("act", Ut[:, a, g4 * 4:g4 * 4 + 4, :], bk.rearrange("p (c e) -> p c e", c=4), [], [bb, B_U])
                    fill(1)
            q3 = rqT.rearrange("p (c i) -> p c i", c=16)
            TT("dve", qdL.rearrange("p (c i) -> p c i", c=16), q3, decq[:, 0, h, :].unsqueeze(1).broadcast_to([128, 16, 128]), ALU.mult, [B_q, B_dq], [B_ql])
            TT("dve", qdR.rearrange("p (c i) -> p c i", c=16), q3, decq[:, 1, h, :].unsqueeze(1).broadcast_to([128, 16, 128]), ALU.mult, [B_q, B_dq], [B_qr])
            MS("dve", cur[:, 0, :], 0.0, [B_curs[0]])
            CP("dve", cur[:, 1, :], SRb[:, h, :], [B_srb[h]], [B_curs[1]])
            for c in range(16):
                CP("dve", Sbf[:, 0, c, :], cur[:, 0, :], [B_curs[0]], [B_Sbs[0]])
                STT("dve", cur[:, 0, :], cur[:, 0, :], gCL[:, h:h + 1], Ut[:, 0, c, :], ALU.mult, ALU.add, [B_U, B_sm], [B_curs[0]])
                c2 = 15 - c
                CP("dve", Sbf[:, 1, c2, :], cur[:, 1, :], [B_curs[1]], [B_Sbs[1]])
                STT("dve", cur[:, 1, :], cur[:, 1, :], gCR[:, h:h + 1], Ut[:, 1, c2, :], ALU.mult, ALU.add, [B_U, B_sm], [B_curs[1]])
            fill(16)

        def ret_B2(h):
            s_ = h % 2
            rqT, B_q = RQ[s_][0].bf(), RQ[s_][1]
            rkT, B_k = RK[s_][0].bf(), RK[s_][1]
            rsg, B_sg = RSG[s_][0].bf(), RSG[s_][1]
            MS("dve", ss[:, 0:16], 0.0, [B_ss])
            for g4 in range(4):
                bi, bk, bb = psum([2, 3])
                for c4 in range(4):
                    c = g4 * 4 + c4
                    MM(bk[:, c4 * 128:(c4 + 1) * 128], rkT[:, c * 128:(c + 1) * 128], rqT[:, c * 128:(c + 1) * 128], True, True, [B_k, B_q], [bb], c4 == 3)
                pa = g4 % 2
                TT("dve", PTt[:, pa, :], bk, MT[:, h, :], ALU.mult, [B_MT], [bb, B_ptl[pa]])
                fill(3)
                oi = [4, 5][g4 % 2]
                ob, obb = banks[oi][:], PB[oi]
                for c4 in range(4):
                    c = g4 * 4 + c4
                    osl = ob[:, c4 * 128:(c4 + 1) * 128]
                    MM(osl, PTt[:, pa, c4 * 128:(c4 + 1) * 128], vtm[:, c, :], True, False, [B_ptl[pa], B_vt], [obb], False)
                    MM(osl, qdL[:, c * 128:(c + 1) * 128], Sbf[:, 0, c, :], False, False, [B_ql, B_Sbs[0]], [obb], False)
                    MM(osl, qdR[:, c * 128:(c + 1) * 128], Sbf[:, 1, c, :], False, True, [B_qr, B_Sbs[1]], [obb], c4 == 3)
                for c4 in range(4):
                    c = g4 * 4 + c4
                    ACT(rjunk, ob[:, c4 * 128:(c4 + 1) * 128], AF.Square, [], [obb, B_junk, B_ss], scale=float(128 ** -0.5), accum=ss[:, c:c + 1])
                ACT(ss[:, 16 + g4 * 4:20 + g4 * 4], ss[:, g4 * 4:g4 * 4 + 4], AF.Ln, [B_eps], [B_ss], bias=epsb)
                ACT(ss[:, 32 + g4 * 4:36 + g4 * 4], ss[:, 16 + g4 * 4:20 + g4 * 4], AF.Exp, [], [B_ss], scale=-0.5)
                for c4 in range(4):
                    c = g4 * 4 + c4
                    ACT(on[:, c, :], ob[:, c4 * 128:(c4 + 1) * 128], AF.Copy, [B_ss], [obb, B_on], scale=ss[:, 32 + c:33 + c])
                fill(3)
            for half in range(2):
                bi, bk, bb = psum([6, 7])
                bkb = bk.bitcast(BF16)
                for c in range(8):
                    cc = half * 8 + c
                    TR(bkb[:, c * 128:(c + 1) * 128], on[:, cc, :], identb, [B_on, B_idb], [bb], c == 7)
                STT("dve", mr[:, half * 1024:(half + 1) * 1024], bkb, pp[:, 32 + h:33 + h], rsg[:, half * 1024:(half + 1) * 1024], ALU.mult, ALU.mult, [B_pp, B_sg], [bb, B_mr])
                fill(2)
            spill(8 + h, mr, B_mr, 0, 2048, "mr")
            fill(100)

        for p_ in ret_A(0):
            p_((0, 1, 2, 3))
        for h in range(NH):
            fillq["q"] = ret_A(h + 1) if h + 1 < NH else []
            ret_B1(h)
            ret_B2(h)
        for r in [x_[0] for x_ in RQ + RK + RV + RSG] + [R_e, R_ql, R_qr, R_kl, R_kr, R_vt, R_U, R_Sb, R_cur, R_pt, R_ss, R_on, R_junk, R_mr]:
            AR.free(r)

    for r in W4["regs"]:
        AR.free(r)
    for r in (R_MT, R_dq, R_nT):
        AR.free(r)
    AR.free(R_srb)
    R_hT = AR.alloc(KC * 1024 * 4, "hT"); hT = R_hT.f32().rearrange("p (c t) -> p c t", c=KC)
    B_hT = [R_hT.buf("hT%d" % tg) for tg in range(2)]
    R_32 = AR.alloc(KC * 1024 * 2, "r32"); r32 = R_32.bf().rearrange("p (c t) -> p c t", c=KC)
    B_32 = [R_32.buf("r32_%d" % tg) for tg in range(2)]
    R_rbc = AR.alloc(1024 * 4, "rbc"); rbc = R_rbc.f32(); B_rbc = [R_rbc.buf("rbc%d" % tg) for tg in range(2)]
    R_xy = [AR.alloc(D * 4, "xy%d" % i) for i in range(2)]
    xy = [r.f32() for r in R_xy]; B_xy = [r.buf() for r in R_xy]
    R_at = [AR.alloc(4 * 1024 * 2, "actT%d" % i) for i in range(2)]
    actT = [r.bf().rearrange("p (j t) -> p j t", j=4) for r in R_at]; B_at = [r.buf() for r in R_at]
    R_sgt = [AR.alloc(512 * 2, "sgt%d" % i) for i in range(2)]
    sgt = [r.bf() for r in R_sgt]; B_sgt = [r.buf() for r in R_sgt]
    W8b = make_wslots(4, 256, "w8b_")
    R_wd = [AR.alloc(4 * D * 2, "wd%d" % i) for i in range(2)]
    wd = [r.bf().rearrange("p (j n) -> p j n", j=4) for r in R_wd]; B_wd = [r.buf() for r in R_wd]
    wd_n = {"n": 0}
    B_y = [Buf("y%d" % i) for i in range(16)]
    mix_v = mix_d.rearrange("h p t -> p h t")
    sgn = {"n": 0}

    def feat_norm(nw_col0, final):
        for tg in range(2):
            if final:
                sqv = R_wd[tg].bf().rearrange("p (c t) -> p c t", c=KC)
                sqb = B_wd[tg]
            else:
                sqv = r32[:, :, tg * 512:(tg + 1) * 512]
                sqb = B_32[tg]
            ACT(sqv, hT[:, :, tg * 512:(tg + 1) * 512], AF.Square, [B_hT[tg]], [sqb])
            bi, bk, bb = psum([6, 7])
            for c in range(KC):
                MM(bk, onesb, sqv[:, c, :], c == 0, c == KC - 1, [B_ones, sqb], [bb], c == KC - 1)
            ACT(rbc[:, tg * 512:(tg + 1) * 512], bk, AF.Ln, [B_eps], [bb, B_rbc[tg]], scale=float(1.0 / D), bias=epsb)
            ACT(rbc[:, tg * 512:(tg + 1) * 512], rbc[:, tg * 512:(tg + 1) * 512], AF.Exp, [], [B_rbc[tg]], scale=-0.5)
        for tg in range(2):
            for c in range(KC):
                if final:
                    STT("dve", hT[:, c, tg * 512:(tg + 1) * 512], hT[:, c, tg * 512:(tg + 1) * 512], pp[:, nw_col0 + c:nw_col0 + c + 1],
                        rbc[:, tg * 512:(tg + 1) * 512], ALU.mult, ALU.mult, [B_pp, B_rbc[tg]], [B_hT[tg]])
                else:
                    STT("dve", r32[:, c, tg * 512:(tg + 1) * 512], hT[:, c, tg * 512:(tg + 1) * 512], pp[:, nw_col0 + c:nw_col0 + c + 1],
                        rbc[:, tg * 512:(tg + 1) * 512], ALU.mult, ALU.mult, [B_pp, B_rbc[tg], B_hT[tg]], [B_32[tg]])

    dbank = {"n": 0}

    def down_group(g, aslot, wslot):
        pairs = [(cb, tg) for cb in range(KC) for tg in range(2)]
        for pi in range(0, len(pairs), 2):
            bsel = []
            for _ in range(2):
                bsel.append([4, 5, 6, 7][dbank["n"] % 4])
                dbank["n"] += 1
            S.prewait("pe", [PB[b_] for b_ in bsel])
            for (cb, tg), b_ in zip(pairs[pi:pi + 2], bsel):
                bk, bb = banks[b_][:], PB[b_]
                for jj in range(4):
                    MM(bk, wd[wslot][:, jj, cb * 128:(cb + 1) * 128], actT[aslot][:, jj, tg * 512:(tg + 1) * 512], jj == 0, jj == 3,
                       [B_wd[wslot], B_at[aslot]], [bb], jj == 3)
                TT("dve", hT[:, cb, tg * 512:(tg + 1) * 512], hT[:, cb, tg * 512:(tg + 1) * 512], bk, ALU.add, [], [bb, B_hT[tg]])

    def prologue_dma(tt):
        tok0 = tt * 1024
        all_mix = [B_mixd[i][tt] for i in range(16)]
        S.dma("sp", r32, mix_v[:, :, tok0:tok0 + 1024], "r32", reads=all_mix, writes=B_32)
        for s in range(2):
            S.dma("sp", xy[s], x[tok0 + s * 128: tok0 + (s + 1) * 128, :], "xy%d" % s, writes=[B_xy[s]])

    prologue_dma(0)
    for tt in range(2):
        tok0 = tt * 1024
        for s in range(8):
            sl = s % 2
            if s >= 2:
                S.dma("sp", xy[sl], x[tok0 + s * 128: tok0 + (s + 1) * 128, :], "xy%d" % sl, writes=[B_xy[sl]])
            for g4 in range(4):
                bi, bk, bb = psum([6, 7])
                for c4 in range(4):
                    cc = g4 * 4 + c4
                    TR(bk[:, c4 * 128:(c4 + 1) * 128], xy[sl][:, cc * 128:(cc + 1) * 128], identf, [B_xy[sl], B_idf], [bb], c4 == 3)
                CP(alt(), hT[:, g4 * 4:g4 * 4 + 4, s * 128:(s + 1) * 128], bk.rearrange("p (c t) -> p c t", c=4), [], [bb, B_hT[s // 4]])
        for u in range(8):
            wap, wbuf = wload(W8b, w_out, u * 256, 256)
            for cb2 in range(2):
                for tg in range(2):
                    bi, bk, bb = psum([0, 1, 2, 3])
                    for c in range(KC):
                        MM(bk, wap[:, c, cb2 * 128:(cb2 + 1) * 128], r32[:, c, tg * 512:(tg + 1) * 512], c == 0, c == KC - 1,
                           [wbuf, B_32[tg]], [bb], c == KC - 1)
                    cb = u * 2 + cb2
                    TT("dve", hT[:, cb, tg * 512:(tg + 1) * 512], hT[:, cb, tg * 512:(tg + 1) * 512], bk, ALU.add, [], [bb, B_hT[tg]])
        if "h1" in dbg and tt == 0:
            pass
        feat_norm(0, False)
        prev = None
        for g in range(11):
            aslot = g % 2
            wslot = wd_n["n"] % 2
            wd_n["n"] += 1
            units = []
            for a in range(2):
                col0 = (g * 4 + a * 2) * 128
                units.append((wload(W8b, w_gate, col0, 256), wload(W8b, w_up, col0, 256)))
            srcd = w_down[g * 512:(g + 1) * 512, :].rearrange("(j p) n -> p j n", p=128)
            S.dma("pool", wd[wslot], srcd, "wd%d" % wslot, writes=[B_wd[wslot]])
            for jj in range(4):
                (wg_ap, wg_b), (wu_ap, wu_b) = units[jj // 2]
                cb2 = jj % 2
                for tg in range(2):
                    gi, gk, gb = psum([0, 1, 2, 3])
                    ui, uk, ub = psum([0, 1, 2, 3])
                    S.prewait("pe", [gb, ub])
                    for c in range(KC):
                        MM(gk, wg_ap[:, c, cb2 * 128:(cb2 + 1) * 128], r32[:, c, tg * 512:(tg + 1) * 512], c == 0, c == KC - 1,
                           [wg_b, B_32[tg]], [gb], c == KC - 1)
                    for c in range(KC):
                        MM(uk, wu_ap[:, c, cb2 * 128:(cb2 + 1) * 128], r32[:, c, tg * 512:(tg + 1) * 512], c == 0, c == KC - 1,
                           [wu_b, B_32[tg]], [ub], c == KC - 1)
                    ss_ = sgn["n"] % 2
                    sgn["n"] += 1
                    ACT(sgt[ss_], gk, AF.Silu, [], [gb, B_sgt[ss_]])
                    TT("dve", actT[aslot][:, jj, tg * 512:(tg + 1) * 512], uk, sgt[ss_], ALU.mult, [B_sgt[ss_]], [ub, B_at[aslot]])
            if prev is not None:
                down_group(*prev)
            prev = (g, aslot, wslot)
        down_group(*prev)
        if tt + 1 < 2:
            prologue_dma(tt + 1)
        feat_norm(16, True)
        for s in range(8):
            sl = s % 2
            yst = R_at[sl].f32()
            for g4 in range(4):
                bi, bk, bb = psum([6, 7])
                for c4 in range(4):
                    cc = g4 * 4 + c4
                    TR(bk[:, c4 * 128:(c4 + 1) * 128], hT[:, cc, s * 128:(s + 1) * 128], identf, [B_hT[s // 4], B_idf], [bb], c4 == 3)
                CP(alt(), yst[:, g4 * 512:(g4 + 1) * 512], bk, [], [bb, B_at[sl]])
            S.dma("sp", y[tok0 + s * 128: tok0 + (s + 1) * 128, :], yst, "yst%d" % sl, reads=[B_at[sl]], writes=[B_y[tt * 8 + s]])
    S.wait_all("sp", B_y)
    if "mixT" in dbg:
        S.wait_all("sp", [b for bb_ in B_mixd for b in bb_])
    print("arena peak (KiB):", AR.peak * 2 / 1024.0, "insts:", S.n_inst)
    S.emit()
    return nc


def make_in_maps(inputs, cores=range(8)):
    f = lambda k: np.ascontiguousarray(np.asarray(inputs[k], dtype=np.float32))
    x = f("x")
    w_in, w_out, w_gate, w_up, w_down = f("w_in")[0], f("w_out")[0], f("w_gate")[0], f("w_up")[0], f("w_down")[0]
    nmix, nffn, nfin = f("norm_mix_w")[0], f("norm_ffn_w")[0], f("norm_final_w")
    retw = f("ret_norm_w")[0]
    dfw, dbw = f("ret_decay_fwd")[0], f("ret_decay_bwd")[0]
    cst = host_consts()
    idxm = np.arange(2048, dtype=np.float32)
    pp = np.concatenate([nffn.reshape(16, 128).T, nfin.reshape(16, 128).T, retw.reshape(8, 128).T], axis=1)
    pp = np.ascontiguousarray(pp, dtype=np.float32)
    maps = []
    for c in cores:
        b, hf = c // 2, c % 2
        xb = x[b] if hf == 0 else x[b][::-1]
        dL, dR = (dfw, dbw) if hf == 0 else (dbw, dfw)
        bcv = np.concatenate([nmix, dL, dR]).astype(np.float32)
        maps.append({"x": np.ascontiguousarray(xb), "w_in": w_in, "w_out": w_out, "w_gate": w_gate, "w_up": w_up,
                     "w_down": w_down, "cst": cst, "pp": pp, "bcv": bcv, "idxm": idxm})
    return maps


def kernel(**inputs):
    nc = build_program()
    maps = make_in_maps(inputs)
    res = run_bass_kernel_spmd(nc, maps, core_ids=list(range(8)))
    out = np.empty((4, 4096, 2048), np.float32)
    for c in range(8):
        b, hf = c // 2, c % 2
        yc = np.asarray(res.results[c]["y"], dtype=np.float32)
        if hf == 0:
            out[b, :2048] = yc
        else:
            out[b, 2048:] = yc[::-1]
    return out
```

```python
import numpy as np
import ml_dtypes
import concourse.bass as bass
import concourse.mybir as mybir
from concourse.bass_utils import run_bass_kernel_spmd

F32 = mybir.dt.float32
BF16 = mybir.dt.bfloat16
AF = mybir.ActivationFunctionType
ALU = mybir.AluOpType
AX = mybir.AxisListType

ENGS = ("pe", "act", "dve", "pool", "sp")


class Buf:
    def __init__(self, name):
        self.name = name
        self.w = None
        self.r = {}


class Sched:
    def __init__(self, nc):
        self.nc = nc
        self.ops = {e: [] for e in ENGS}
        self.cnt = {e: 0 for e in ENGS}
        self.known = {e: {} for e in ENGS}
        self.dma_cnt = {}
        self.n_inst = {e: 0 for e in ENGS}

    def _deps(self, eng, reads, writes):
        deps = []
        for b in reads:
            if b.w is not None:
                deps.append(b.w)
        for b in writes:
            if b.w is not None:
                deps.append(b.w)
            deps.extend(b.r.items())
        need = {}
        for key, val in deps:
            if key == "pe" and eng == "pe":
                continue
            if self.known[eng].get(key, 0) >= val:
                continue
            if need.get(key, 0) < val:
                need[key] = val
        for key, val in need.items():
            self.known[eng][key] = val
            self.ops[eng].append(("wait", key, val))

    def _record(self, ev, reads, writes):
        key, val = ev
        for b in reads:
            if b.r.get(key, 0) < val:
                b.r[key] = val
        for b in writes:
            b.w = ev
            b.r = {}

    def op(self, eng, fn, reads=(), writes=(), signal=True):
        self._deps(eng, reads, writes)
        if signal:
            self.cnt[eng] += 1
            ev = (eng, self.cnt[eng])
        else:
            ev = (eng, self.cnt[eng] + 1)
        self.ops[eng].append(("op", fn, signal))
        self.n_inst[eng] += 1
        self._record(ev, reads, writes)

    def dma(self, eng, out_ap, in_ap, key, reads=(), writes=()):
        self._deps(eng, reads, writes)
        k = ("dma", key)
        self.dma_cnt[k] = self.dma_cnt.get(k, 0) + 16
        ev = (k, self.dma_cnt[k])
        self.ops[eng].append(("dma", out_ap, in_ap, k))
        self.n_inst[eng] += 1
        self._record(ev, reads, writes)

    def prewait(self, eng, bufs):
        self._deps(eng, [], bufs)

    def wait_all(self, eng, bufs):
        self._deps(eng, [], bufs)

    def emit(self):
        nc = self.nc
        from contextlib import ExitStack
        with ExitStack() as st:
            sems = {}
            for e in ("pe", "act", "dve", "pool"):
                sems[e] = st.enter_context(nc.semaphore("s_" + e))
            for i, k in enumerate(sorted(self.dma_cnt.keys(), key=str)):
                sems[k] = st.enter_context(nc.semaphore("d%d" % i))
            block = st.enter_context(nc.Block())

            def run(eng_name, eng):
                for o in self.ops[eng_name]:
                    if o[0] == "wait":
                        eng.wait_ge(sems[o[1]], o[2])
                    elif o[0] == "op":
                        ins = o[1](eng)
                        if o[2]:
                            ins.then_inc(sems[eng_name], 1)
                    else:
                        eng.dma_start(out=o[1], in_=o[2]).then_inc(sems[o[3]], 16)

            @block.tensor
            def _(eng):
                run("pe", eng)

            @block.scalar
            def _(eng):
                run("act", eng)

            @block.vector
            def _(eng):
                run("dve", eng)

            @block.gpsimd
            def _(eng):
                run("pool", eng)

            @block.sync
            def _(eng):
                run("sp", eng)


class Region:
    def __init__(self, arena, off, size, name, prior, nbytes):
        self.arena, self.off, self.size, self.name = arena, off, size, name
        self.n = nbytes // 2
        self.prior = prior
        self.bufs = []

    def buf(self, name=None):
        b = Buf(name or self.name)
        b.r = dict(self.prior)
        self.bufs.append(b)
        return b

    def bf(self):
        return self.arena.t[:, self.off:self.off + self.n]

    def f32(self):
        return self.arena.t[:, self.off:self.off + self.n].bitcast(F32)


class Arena:
    def __init__(self, tensor, total):
        self.t, self.total = tensor, total
        self.live = []
        self.dead = []
        self.peak = 0

    def alloc(self, nbytes, name):
        size = ((nbytes + 63) // 64) * 32
        self.live.sort(key=lambda r: r.off)
        off = 0
        for r in self.live:
            if r.off - off >= size:
                break
            off = max(off, r.off + r.size)
        if off + size > self.total:
            raise RuntimeError("arena OOM for %s (%d B); live=%s" % (
                name, nbytes, [(r.name, r.size * 2) for r in self.live]))
        prior = {}
        for (o, sz, ev) in self.dead:
            if o < off + size and off < o + sz:
                for k, v in ev.items():
                    if prior.get(k, 0) < v:
                        prior[k] = v
        reg = Region(self, off, size, name, prior, nbytes)
        self.live.append(reg)
        self.peak = max(self.peak, off + size)
        return reg

    def free(self, reg):
        self.live.remove(reg)
        ev = dict(reg.prior)
        for b in reg.bufs:
            items = list(b.r.items())
            if b.w is not None:
                items.append(b.w)
            for k, v in items:
                if ev.get(k, 0) < v:
                    ev[k] = v
        self.dead.append((reg.off, reg.size, ev))


D = 2048
T_OWN = 2048
KC = 16
DFF = 5632
NH = 8
EPS = 1e-6
C_ID, C_DPOS, C_DNEG, C_IQ1, C_IQR, C_PL, C_PR, C_ABSD, C_VALID, NCST = 0, 128, 256, 384, 512, 640, 641, 642, 898, 1154
MERGE_DEN = True
NXT = 4
ARENA_ELEMS = 105472


def host_consts():
    c = np.zeros((128, NCST), np.float32)
    p = np.arange(128)[:, None].astype(np.float32)
    i = np.arange(128)[None, :].astype(np.float32)
    c[:, C_ID:C_ID + 128] = np.eye(128, dtype=np.float32)
    c[:, C_DPOS:C_DPOS + 128] = np.maximum(i - p, 0)
    c[:, C_DNEG:C_DNEG + 128] = np.maximum(p - i, 0)
    c[:, C_IQ1:C_IQ1 + 128] = i + 1
    c[:, C_IQR:C_IQR + 128] = 128 - i
    c[:, C_PL] = 127 - p[:, 0]
    c[:, C_PR] = p[:, 0]
    cc = np.arange(256)[None, :].astype(np.float32)
    c[:, C_ABSD:C_ABSD + 256] = np.abs(cc - 64 - p)
    c[:, C_VALID:C_VALID + 256] = ((cc >= p) & (cc <= p + 128)).astype(np.float32)
    return c


def build_program(dbg=()):
    nc = bass.Bass("TRN2", target_bir_lowering=False)
    dt = nc.dram_tensor
    x = dt("x", [4096, D], F32, kind="ExternalInput").ap()
    w_in = dt("w_in", [D, 7168], F32, kind="ExternalInput").ap()
    w_out = dt("w_out", [D, D], F32, kind="ExternalInput").ap()
    w_gate = dt("w_gate", [D, DFF], F32, kind="ExternalInput").ap()
    w_up = dt("w_up", [D, DFF], F32, kind="ExternalInput").ap()
    w_down = dt("w_down", [DFF, D], F32, kind="ExternalInput").ap()
    cst_d = dt("cst", [128, NCST], F32, kind="ExternalInput").ap()
    pp_d = dt("pp", [128, 40], F32, kind="ExternalInput").ap()
    bcv_d = dt("bcv", [D + 16], F32, kind="ExternalInput").ap()
    idxm_d = dt("idxm", [2048], F32, kind="ExternalInput").ap()
    y = dt("y", [T_OWN, D], F32, kind="ExternalOutput").ap()
    mix_d = dt("mixT", [16, 128, T_OWN], BF16, kind=("ExternalOutput" if "mixT" in dbg else "Internal")).ap()
    dbg_out = {}

    S = Sched(nc)
    arena_t = nc.alloc_sbuf_tensor("arena", [128, ARENA_ELEMS], BF16)
    AR = Arena(arena_t, ARENA_ELEMS)
    banks = [nc.alloc_psum_tensor("bank%d" % i, [128, 512], F32) for i in range(8)]
    PB = [Buf("bank%d" % i) for i in range(8)]
    bank_rr = {"n": 0}

    def psum(which=None):
        if which is None:
            which = range(8)
        which = list(which)
        i = which[bank_rr["n"] % len(which)]
        bank_rr["n"] += 1
        return i, banks[i][:], PB[i]

    def MM(out, lhsT, rhs, start, stop, reads, writes, signal, skip=False):
        S.op("pe", lambda e: e.matmul(out, lhsT=lhsT, rhs=rhs, start=start, stop=stop, skip_group_check=skip),
             reads, writes, signal)

    def TR(out, in_, ident, reads, writes, signal):
        S.op("pe", lambda e: e.transpose(out, in_, ident), reads, writes, signal)

    def ACT(out, in_, func, reads, writes, scale=1.0, bias=None, accum=None):
        kw = {}
        if bias is not None:
            kw["bias"] = bias
        if accum is not None:
            kw["accum_out"] = accum
        S.op("act", lambda e: e.activation(out, in_, func, scale=scale, **kw), reads, writes)

    def CP(eng, out, in_, reads, writes):
        if eng == "act":
            S.op("act", lambda e: e.activation(out, in_, AF.Copy), reads, writes)
        else:
            S.op(eng, lambda e: e.tensor_copy(out, in_), reads, writes)

    def TT(eng, out, a, b, op, reads, writes):
        S.op(eng, lambda e: e.tensor_tensor(out, a, b, op=op), reads, writes)

    def STT(eng, out, in0, scalar, in1, op0, op1, reads, writes):
        S.op(eng, lambda e: e.scalar_tensor_tensor(out, in0, scalar, in1, op0=op0, op1=op1), reads, writes)

    def TS(eng, out, in0, s1, s2, op0, op1, reads, writes):
        if s2 is None:
            S.op(eng, lambda e: e.tensor_scalar(out, in0, s1, None, op0=op0), reads, writes)
        else:
            S.op(eng, lambda e: e.tensor_scalar(out, in0, s1, s2, op0=op0, op1=op1), reads, writes)

    def MS(eng, out, val, writes):
        S.op(eng, lambda e: e.memset(out, val), (), writes)

    def RECIP(out, in_, reads, writes):
        S.op("dve", lambda e: e.reciprocal(out, in_), reads, writes)

    rr = {"n": 0}

    def alt():
        rr["n"] += 1
        return "act" if rr["n"] % 2 else "dve"

    R_idf = AR.alloc(128 * 4, "identf"); identf = R_idf.f32(); B_idf = R_idf.buf()
    R_idb = AR.alloc(128 * 2, "identb"); identb = R_idb.bf(); B_idb = R_idb.buf()
    R_ones = AR.alloc(128 * 2, "ones"); onesb = R_ones.bf(); B_ones = R_ones.buf()
    R_pp = AR.alloc(40 * 4, "pp"); pp = R_pp.f32(); B_pp = R_pp.buf()
    R_eps = AR.alloc(64, "eps"); epsb = R_eps.f32()[:, 0:1]; oneb = R_eps.f32()[:, 1:2]; B_eps = R_eps.buf()
    R_sm = AR.alloc(64 * 4, "small"); sm = R_sm.f32(); B_sm = R_sm.buf()
    lgL, lgR, kdL, kdR, gCL, gCR = (sm[:, 8 * i:8 * i + 8] for i in range(6))
    R_srb = AR.alloc(NH * 128 * 4, "SRb"); SRb = R_srb.f32().rearrange("p (h e) -> p h e", h=NH); B_srb = [R_srb.buf("SRb%d" % h) for h in range(NH)]

    R_bcv = AR.alloc((D + 16) * 4, "bcv"); bcv = R_bcv.f32(); B_bcv = R_bcv.buf()
    R_E = AR.alloc(24 * 256 * 2, "alibiE"); Emask = R_E.bf().rearrange("p (m c) -> p m c", m=24); B_E = R_E.buf()
    R_halo = AR.alloc(2 * NH * 1024 * 2, "halo"); halo = R_halo.bf().rearrange("p (a h t) -> p a h t", a=2, h=NH)
    B_halo = [[R_halo.buf("halo%d_%d" % (a, h)) for h in range(NH)] for a in range(2)]
    R_nT = AR.alloc(KC * 2048 * 2, "nT"); nT = R_nT.bf().rearrange("p (c t) -> p c t", c=KC)
    B_nT = [R_nT.buf("nT%d" % g) for g in range(4)]

    R_cst = AR.alloc(NCST * 4, "cst"); cst = R_cst.f32(); B_cst = R_cst.buf()
    S.dma("sp", cst, cst_d, "cst", writes=[B_cst])
    S.dma("sp", pp, pp_d, "pp", writes=[B_pp])
    S.dma("sp", bcv, bcv_d.partition_broadcast(128), "bcv", writes=[B_bcv])
    CP("dve", identf, cst[:, C_ID:C_ID + 128], [B_cst], [B_idf])
    CP("dve", identb, cst[:, C_ID:C_ID + 128], [B_cst], [B_idb])
    MS("dve", onesb, 1.0, [B_ones])
    MS("dve", epsb, EPS, [B_eps])
    MS("dve", oneb, 1.0, [B_eps])
    ACT(sm[:, 0:16], bcv[:, D:D + 16], AF.Exp, [B_bcv], [B_sm])
    TS("dve", sm[:, 0:16], sm[:, 0:16], -1.0, None, ALU.mult, None, [B_sm], [B_sm])
    ACT(kdL, lgL, AF.Exp, [B_sm, B_cst], [B_sm], scale=cst[:, C_PL:C_PL + 1])
    ACT(kdR, lgR, AF.Exp, [B_sm, B_cst], [B_sm], scale=cst[:, C_PR:C_PR + 1])
    ACT(sm[:, 32:48], sm[:, 0:16], AF.Exp, [B_sm], [B_sm], scale=128.0)
    R_tmp = AR.alloc(512 * 4, "setup_tmp"); tmpf = R_tmp.f32(); B_tmp = R_tmp.buf()
    for d_i, dil in enumerate((1, 4, 16)):
        for h in range(NH):
            slope = 2.0 ** (-(h + 1))
            ACT(tmpf[:, 0:256], cst[:, C_ABSD:C_ABSD + 256], AF.Exp, [B_cst], [B_tmp], scale=-slope * dil)
            TT("dve", Emask[:, d_i * NH + h, :], tmpf[:, 0:256], cst[:, C_VALID:C_VALID + 256], ALU.mult, [B_tmp, B_cst], [B_E])
    AR.free(R_tmp)
    AR.free(R_cst)

    def norm_tokens(tok0):
        R_xt = [AR.alloc(D * 4, "xt%d" % i) for i in range(NXT)]
        R_nb = [AR.alloc(D * 2, "nb%d" % i) for i in range(3)]
        R_junk = AR.alloc(D * 2, "junk")
        R_sts = [AR.alloc(64, "nstat%d" % i) for i in range(2)]
        xt = [r.f32() for r in R_xt]; Bxt = [r.buf() for r in R_xt]
        nb = [r.bf() for r in R_nb]; Bnb = [r.buf() for r in R_nb]
        junk = R_junk.bf(); Bjunk = R_junk.buf()
        def stage_a(s):
            sl = s % 2
            st = R_sts[sl].f32(); Bst = R_sts[sl].bufs[0] if R_sts[sl].bufs else R_sts[sl].buf()
            xs = s % NXT
            ns_ = s % 3
            S.dma("sp", xt[xs], x[tok0 + s * 128: tok0 + (s + 1) * 128, :], "xt%d" % xs, writes=[Bxt[xs]])
            MS("dve", st[:, 0:1], 0.0, [Bst])
            ACT(junk, xt[xs], AF.Square, [Bxt[xs]], [Bjunk, Bst], scale=float(D ** -0.5), accum=st[:, 0:1])
            ACT(st[:, 1:2], st[:, 0:1], AF.Ln, [Bst, B_eps], [Bst], bias=epsb)
            ACT(st[:, 2:3], st[:, 1:2], AF.Exp, [Bst], [Bst], scale=-0.5)
            STT("dve", nb[ns_], xt[xs], st[:, 2:3], bcv[:, 0:D], ALU.mult, ALU.mult, [Bxt[xs], Bst, B_bcv], [Bnb[ns_]])

        def stage_b(s):
            ns_ = s % 3
            g = s // 4
            for half in range(2):
                bi, bk, bb = psum([6, 7])
                bkb = bk.bitcast(BF16)
                for c in range(8):
                    cc = half * 8 + c
                    TR(bkb[:, c * 128:(c + 1) * 128], nb[ns_][:, cc * 128:(cc + 1) * 128], identb, [Bnb[ns_], B_idb], [bb], c == 7)
                CP("act" if half == 0 else "dve", nT[:, half * 8:half * 8 + 8, s * 128:(s + 1) * 128], bkb.rearrange("p (c t) -> p c t", c=8), [], [bb, B_nT[g]])
        stage_a(0)
        for s in range(16):
            if s + 1 < 16:
                stage_a(s + 1)
            stage_b(s)
        for r in R_xt + R_nb + [R_junk] + R_sts:
            AR.free(r)

    def make_wslots(n, ncols, name):
        regs = [AR.alloc(KC * ncols * 2, "%s%d" % (name, i)) for i in range(n)]
        return {"regs": regs, "aps": [r.bf().rearrange("p (c n) -> p c n", c=KC) for r in regs],
                "bufs": [r.buf() for r in regs], "n": 0, "name": name}

    def wload(ws, w_ap, col0, ncols):
        i = ws["n"] % len(ws["regs"])
        ws["n"] += 1
        src = w_ap.rearrange("(c p) n -> p c n", p=128)[:, :, col0:col0 + ncols]
        S.dma("pool", ws["aps"][i][:, :, 0:ncols], src, "%s%d" % (ws["name"], i), writes=[ws["bufs"][i]])
        return ws["aps"][i], ws["bufs"][i]

    def proj_block(wap, wbuf, wc0, tgs, evac, banks_sel=(0, 1, 2, 3)):
        for tg in tgs:
            bi, bk, bb = psum(banks_sel)
            for c in range(KC):
                MM(bk, wap[:, c, wc0:wc0 + 128], nT[:, c, tg * 512:(tg + 1) * 512], c == 0, c == KC - 1,
                   [wbuf, B_nT[tg]], [bb], c == KC - 1)
            evac(tg, bk, bb)

    norm_tokens(2048)
    W8 = make_wslots(3, 256, "w8_")
    for a, cbase in ((0, 1024), (1, 2048)):
        for u in range(4):
            wap, wbuf = wload(W8, w_in, cbase + u * 256, 256)
            for hh in range(2):
                h = 2 * u + hh

                def ev(tg, bk, bb, a=a, h=h):
                    CP(alt(), halo[:, a, h, tg * 512:(tg + 1) * 512], bk, [], [bb, B_halo[a][h]])
                proj_block(wap, wbuf, hh * 128, (0, 1), ev)

    R_idx = AR.alloc(2048 * 4, "idxm"); idxm = R_idx.f32(); B_idx = R_idx.buf()
    S.dma("sp", idxm, idxm_d.partition_broadcast(128), "idxm", writes=[B_idx])
    R_dec = AR.alloc(2048 * 2, "decfull"); decf = R_dec.bf(); B_dec = R_dec.buf()
    R_kT = AR.alloc(2048 * 2, "pre_kT"); pkT = R_kT.bf(); B_pkT = R_kT.buf()
    R_vT = AR.alloc(2048 * 2, "pre_vT"); pvT = R_vT.bf(); B_pvT = R_vT.buf()
    R_ktm = AR.alloc(2048 * 2, "pre_ktm"); pktm = R_ktm.bf().rearrange("p (c d) -> p c d", c=16); B_pktm = R_ktm.buf()
    R_vtm = AR.alloc(2048 * 2, "pre_vtm"); pvtm = R_vtm.bf().rearrange("p (c d) -> p c d", c=16); B_pvtm = R_vtm.buf()
    for u in range(4):
        wk, wkb = wload(W8, w_in, 4096 + u * 256, 256)
        wv, wvb = wload(W8, w_in, 5120 + u * 256, 256)
        for hh in range(2):
            h = 2 * u + hh
            ACT(decf, idxm, AF.Exp, [B_idx, B_sm], [B_dec], scale=lgR[:, h:h + 1])

            def evk(tg, bk, bb):
                TT("dve", pkT[:, tg * 512:(tg + 1) * 512], bk, decf[:, tg * 512:(tg + 1) * 512], ALU.mult, [B_dec], [bb, B_pkT])

            def evv(tg, bk, bb):
                CP("act", pvT[:, tg * 512:(tg + 1) * 512], bk, [], [bb, B_pvT])
            proj_block(wk, wkb, hh * 128, range(4), evk)
            proj_block(wv, wvb, hh * 128, range(4), evv)
            for (srcT, Bsrc, dst, Bdst) in ((pkT, B_pkT, pktm, B_pktm), (pvT, B_pvT, pvtm, B_pvtm)):
                for half in range(2):
                    bi, bk, bb = psum([6, 7])
                    bkb = bk.bitcast(BF16)
                    for c in range(8):
                        cc = half * 8 + c
                        TR(bkb[:, c * 128:(c + 1) * 128], srcT[:, cc * 128:(cc + 1) * 128], identb, [Bsrc, B_idb], [bb], c == 7)
                    CP(alt(), dst[:, half * 8:half * 8 + 8, :], bkb.rearrange("p (c t) -> p c t", c=8), [], [bb, Bdst])
            bi, bk, bb = psum([4, 5])
            for c in range(16):
                MM(bk[:, 0:128], pktm[:, c, :], pvtm[:, c, :], c == 0, c == 15, [B_pktm, B_pvtm], [bb], c == 15)
            CP("dve", SRb[:, h, :], bk[:, 0:128], [], [bb, B_srb[h]])
    for r in (R_idx, R_dec, R_kT, R_vT, R_ktm, R_vtm):
        AR.free(r)

    for r in W8["regs"]:
        AR.free(r)
    norm_tokens(0)
    W4 = make_wslots(6, 128, "w4_")

    B_mixd = [[Buf("mixd%d_%d" % (i, j)) for j in range(2)] for i in range(16)]

    def spill(h16, src_ap, src_buf, col0, ncols, key):
        wr = [B_mixd[h16][col0 // 1024]] if ncols == 1024 else B_mixd[h16]
        S.dma("sp", mix_d[h16, :, col0:col0 + ncols], src_ap, key, reads=[src_buf], writes=wr)

    AR.free(R_bcv)
    if "skip_attn" not in dbg:
        R_QKV = [[AR.alloc(2048 * 2, "%sT%d" % (nm, i)) for nm in "QKV"] for i in range(2)]
        QKVap = [[r.bf() for r in rr_] for rr_ in R_QKV]
        QKVb = [[r.buf() for r in rr_] for rr_ in R_QKV]
        R_Vt = [AR.alloc(32 * 128 * 2, "Vtm%d" % i) for i in range(2)]
        Vt = [r.bf().rearrange("p (s e) -> p s e", s=32) for r in R_Vt]; B_Vt = [r.buf() for r in R_Vt]
        NP = 4
        R_P = [AR.alloc(2 * 512 * 2, "P%d" % i) for i in range(NP)]
        Pt = [r.bf() for r in R_P]; B_P = [r.buf() for r in R_P]
        R_rc = AR.alloc(1024 * 4, "recip"); rc = R_rc.f32(); B_rc = R_rc.buf()
        R_mo = [AR.alloc(1024 * 2, "mixo%d" % i) for i in range(2)]
        mo = [r.bf() for r in R_mo]; B_mo = [r.buf() for r in R_mo]
        pc = {"n": 0}
        LOOK = 2

        def att_front(st, h):
            QT, KT, VT = QKVap[h % 2]
            B_Q, B_K, B_V = QKVb[h % 2]
            dil, u, n0, n1 = st["dil"], st["u"], st["n0"], st["n1"]
            kp = min(128, st["Lk"] - 128 * u)
            n = n1 - n0
            c0 = n0 - (128 * u - 64)
            G = len(st["rs"])
            bi, bk, bb = psum([2, 3])
            for k_, r_ in enumerate(st["rs"]):
                if u < st["n_own"]:
                    p0 = dil * 128 * u + r_
                    kap = KT[:, p0:p0 + dil * (kp - 1) + 1:dil]; kb_ = B_K
                else:
                    p0 = dil * 128 * u + r_ - 2048
                    kap = halo[:, 0, h, p0:p0 + dil * (kp - 1) + 1:dil]; kb_ = B_halo[0][h]
                q0 = dil * n0 + r_
                qap = QT[:, q0:q0 + dil * (n - 1) + 1:dil]
                MM(bk[0:kp, k_ * n:(k_ + 1) * n], kap, qap, True, True, [kb_, B_Q], [bb], k_ == G - 1)
            ps = pc["n"] % NP
            pc["n"] += 1
            praw = Pt[ps][0:kp, 0:G * n]
            pm = Pt[ps][0:kp, 512:512 + G * n]
            ACT(praw, bk[0:kp, 0:G * n], AF.Exp, [], [bb, B_P[ps]], scale=float(128 ** -0.5))
            em = st["em"]
            if G == 1:
                TT("dve", pm, praw, em[0:kp, c0:c0 + n], ALU.mult, [B_E], [B_P[ps]])
            else:
                TT("dve", pm.rearrange("p (g n) -> p g n", g=G), praw.rearrange("p (g n) -> p g n", g=G),
                   em[0:kp, c0:c0 + n].unsqueeze(1).broadcast_to([kp, G, n]), ALU.mult, [B_E], [B_P[ps]])
            st["ps"], st["kp"], st["n"] = ps, kp, n

        def att_back(st, h):
            dil, u, n0, half, vs = st["dil"], st["u"], st["n0"], st["half"], st["vs"]
            kp, n, ps = st["kp"], st["n"], st["ps"]
            pm = Pt[ps][0:kp, 512:1024]
            mms = []
            G = len(st["rs"])
            r0 = st["rs"][0]
            pm3 = pm[:, 0:G * n].rearrange("p (g n) -> p g n", g=G)
            for part in range(2):
                t00 = dil * n0 - half * 1024
                j0 = max(0, -(-(part * 512 - t00) // dil))
                j1 = min(n - 1, (part * 512 + 511 - t00) // dil)
                if j1 < j0:
                    continue
                cnt = j1 - j0 + 1
                base0 = t00 + dil * j0 - part * 512
                for k_, r_ in enumerate(st["rs"]):
                    tt0 = base0 + r_
                    rhs = pm[:, k_ * n + j0:k_ * n + j0 + cnt]
                    oap = banks[4 + part][:, tt0:tt0 + dil * (cnt - 1) + 1:dil]
                    mms.append((oap, Vt[vs][0:kp, st["slot_of"][(r_, u)], :], rhs, [B_Vt[vs], B_P[ps]], [PB[4 + part]]))
                if MERGE_DEN and dil > 1:
                    dap = banks[6 + part][:, base0:base0 + dil * cnt].rearrange("p (j d) -> p j d", d=dil)[:, :, r0:r0 + G]
                    drhs = pm3[:, :, j0:j0 + cnt].rearrange("p g j -> p j g")
                    mms.append((dap, onesb[0:kp, :], drhs, [B_ones, B_P[ps]], [PB[6 + part]]))
                else:
                    for k_, r_ in enumerate(st["rs"]):
                        tt0 = base0 + r_
                        rhs = pm[:, k_ * n + j0:k_ * n + j0 + cnt]
                        dap = banks[6 + part][:, tt0:tt0 + dil * (cnt - 1) + 1:dil]
                        mms.append((dap, onesb[0:kp, :], rhs, [B_ones, B_P[ps]], [PB[6 + part]]))
            for i_, (o_, l_, r__, rd_, wr_) in enumerate(mms):
                MM(o_, l_, r__, False, False, rd_, wr_, i_ == len(mms) - 1, skip=True)

        def proj_units(h):
            QT, KT, VT = QKVap[h % 2]
            B_Q, B_K, B_V = QKVb[h % 2]
            wq, wqb = wload(W4, w_in, h * 128, 128)
            wk, wkb = wload(W4, w_in, 1024 + h * 128, 128)
            wv, wvb = wload(W4, w_in, 2048 + h * 128, 128)
            out = []

            def mk(wap, wbuf, dst, dbuf, eng, tg):
                stt = {}

                def piece(bs, q4):
                    if q4 == 0:
                        stt["b"] = psum(bs)
                    bi, bk, bb = stt["b"]
                    for c in range(q4 * 4, q4 * 4 + 4):
                        MM(bk, wap[:, c, 0:128], nT[:, c, tg * 512:(tg + 1) * 512], c == 0, c == KC - 1,
                           [wbuf, B_nT[tg]], [bb], c == KC - 1)
                    if q4 == 3:
                        CP(eng, dst[:, tg * 512:(tg + 1) * 512], bk, [], [bb, dbuf])
                for q4 in range(4):
                    out.append(lambda bs, q4=q4: piece(bs, q4))
            for tg in range(4):
                mk(wk, wkb, KT, B_K, "dve", tg)
            for tg in range(4):
                mk(wv, wvb, VT, B_V, "act", tg)
            for tg in range(4):
                mk(wq, wqb, QT, B_Q, "act", tg)
            return out

        for u_ in proj_units(0):
            u_((0, 1, 2, 3))
        for h in range(NH):
            QT, KT, VT = QKVap[h % 2]
            B_Q, B_K, B_V = QKVb[h % 2]
            filler = proj_units(h + 1) if h + 1 < NH else []
            nback = 0
            vslot = 0
            for half in range(2):
                for b_ in (4, 5, 6, 7):
                    MS("dve", banks[b_][:], 0.0, [PB[b_]])
                pending = []
                for d_i, dil in enumerate((1, 4, 16)):
                    em = Emask[:, d_i * NH + h, :]
                    Lq = 1024 // dil
                    qa, qb = half * Lq, (half + 1) * Lq
                    Lk = 2048 // dil + 64
                    n_own_tiles = 2048 // (128 * dil)
                    vs = vslot % 2
                    vslot += 1
                    need = []
                    for r_ in range(dil):
                        for u in range((Lk + 127) // 128):
                            n0, n1 = max(qa, 128 * u - 64), min(qb, 128 * u + 192)
                            if n1 > n0:
                                need.append((r_, u, n0, n1))
                    slot_of = {}
                    tiles = sorted(set((r_, u) for (r_, u, _, _) in need))
                    for gi in range(0, len(tiles), 8):
                        grp = tiles[gi:gi + 8]
                        bi, bk, bb = psum([1])
                        bkb = bk.bitcast(BF16)
                        for k_, (r_, u) in enumerate(grp):
                            kp = min(128, Lk - 128 * u)
                            if u < n_own_tiles:
                                p0 = dil * 128 * u + r_
                                src = VT[:, p0:p0 + dil * (kp - 1) + 1:dil]
                                sb_ = B_V
                            else:
                                p0 = dil * 128 * u + r_ - 2048
                                src = halo[:, 1, h, p0:p0 + dil * (kp - 1) + 1:dil]
                                sb_ = B_halo[1][h]
                            TR(bkb[0:kp, k_ * 128:(k_ + 1) * 128], src, identb, [sb_, B_idb], [bb], k_ == len(grp) - 1)
                            slot_of[(r_, u)] = gi + k_
                        CP("dve", Vt[vs][:, gi:gi + len(grp), :], bkb[:, 0:len(grp) * 128].rearrange("p (s e) -> p s e", s=len(grp)), [], [bb, B_Vt[vs]])
                    groups = {}
                    for (r_, u, n0, n1) in need:
                        groups.setdefault((u, n0, n1), []).append(r_)
                    for (u, n0, n1), rs in sorted(groups.items()):
                        G = max(1, min(4, 512 // (n1 - n0)))
                        for gi in range(0, len(rs), G):
                            st = dict(u=u, n0=n0, n1=n1, rs=rs[gi:gi + G], dil=dil, half=half, vs=vs, slot_of=slot_of,
                                      em=em, Lk=Lk, n_own=n_own_tiles)
                            att_front(st, h)
                            pending.append(st)
                            if len(pending) > LOOK:
                                att_back(pending.pop(0), h)
                                nback += 1
                                if filler:
                                    filler.pop(0)((0, 1))
                for st in pending:
                    att_back(st, h)
                mslot = (2 * h + half) % 2
                for part in range(2):
                    ACT(rc[:, part * 512:(part + 1) * 512], banks[6 + part][:], AF.Ln, [], [PB[6 + part], B_rc])
                    ACT(rc[:, part * 512:(part + 1) * 512], rc[:, part * 512:(part + 1) * 512], AF.Exp, [], [B_rc], scale=-1.0)
                    TT("dve", mo[mslot][:, part * 512:(part + 1) * 512], banks[4 + part][:], rc[:, part * 512:(part + 1) * 512], ALU.mult, [B_rc], [PB[4 + part], B_mo[mslot]])
                spill(h, mo[mslot], B_mo[mslot], half * 1024, 1024, "mo%d" % mslot)
            for u_ in filler:
                u_((0, 1))
        for r in [r for rr_ in R_QKV for r in rr_] + [R_rc] + R_Vt + R_P + R_mo:
            AR.free(r)
    AR.free(R_E)
    AR.free(R_halo)

    if "skip_ret" not in dbg:
        R_MT = AR.alloc(NH * 512 * 2, "retMT"); MT = R_MT.bf().rearrange("p (h c) -> p h c", h=NH); B_MT = R_MT.buf()
        R_dq = AR.alloc(2 * NH * 128 * 2, "decq"); decq = R_dq.bf().rearrange("p (a h i) -> p a h i", a=2, h=NH); B_dq = R_dq.buf()
        R_cst = AR.alloc(NCST * 4, "cst2"); cst = R_cst.f32(); B_cst = R_cst.buf()
        S.dma("sp", cst, cst_d, "cst2", writes=[B_cst])
        R_tmp = AR.alloc(512 * 4, "setup_tmp2"); tmpf = R_tmp.f32(); B_tmp = R_tmp.buf()
        for h in range(NH):
            TS("dve", tmpf[:, 0:128], cst[:, C_DPOS:C_DPOS + 128], lgL[:, h:h + 1], None, ALU.mult, None, [B_cst, B_sm], [B_tmp])
            STT("dve", tmpf[:, 128:256], cst[:, C_DNEG:C_DNEG + 128], lgR[:, h:h + 1], tmpf[:, 0:128], ALU.mult, ALU.add, [B_cst, B_sm, B_tmp], [B_tmp])
            for rep in range(4):
                ACT(MT[:, h, rep * 128:(rep + 1) * 128], tmpf[:, 128:256], AF.Exp, [B_tmp], [B_MT])
            ACT(decq[:, 0, h, :], cst[:, C_IQ1:C_IQ1 + 128], AF.Exp, [B_cst, B_sm], [B_dq], scale=lgL[:, h:h + 1])
            ACT(decq[:, 1, h, :], cst[:, C_IQR:C_IQR + 128], AF.Exp, [B_cst, B_sm], [B_dq], scale=lgR[:, h:h + 1])

        AR.free(R_tmp)
        AR.free(R_cst)
        def A2(nbytes, name):
            r = AR.alloc(nbytes, name)
            return r, r.buf()
        RQ = [A2(4096, "r_qT%d" % i) for i in range(2)]
        RK = [A2(4096, "r_kT%d" % i) for i in range(2)]
        RV = [A2(4096, "r_vT%d" % i) for i in range(2)]
        RSG = [A2(4096, "r_sgT%d" % i) for i in range(2)]
        R_e, B_e = A2(512 * 4, "r_e"); re_ = R_e.f32()
        R_ql, B_ql = A2(4096, "r_qdL"); qdL = R_ql.bf()
        R_qr, B_qr = A2(4096, "r_qdR"); qdR = R_qr.bf()
        R_kl, B_kl = A2(4096, "r_kdL"); kdLt = R_kl.bf().rearrange("p (c d) -> p c d", c=16)
        R_kr, B_kr = A2(4096, "r_kdR"); kdRt = R_kr.bf().rearrange("p (c d) -> p c d", c=16)
        R_vt, B_vt = A2(4096, "r_vtm"); vtm = R_vt.bf().rearrange("p (c d) -> p c d", c=16)
        R_U, B_U = A2(2 * 16 * 128 * 4, "r_U"); Ut = R_U.f32().rearrange("p (a c e) -> p a c e", a=2, c=16)
        R_Sb, B_Sb = A2(2 * 16 * 128 * 2, "r_Sbf"); Sbf = R_Sb.bf().rearrange("p (a c e) -> p a c e", a=2, c=16)
        R_cur, B_cur = A2(2 * 128 * 4, "r_cur"); cur = R_cur.f32().rearrange("p (a e) -> p a e", a=2)
        B_curs = [B_cur, R_cur.buf()]
        B_Sbs = [B_Sb, R_Sb.buf()]
        R_pt, B_pt = A2(2 * 512 * 2, "r_PT"); PTt = R_pt.bf().rearrange("p (a c) -> p a c", a=2)
        B_ptl = [B_pt, R_pt.buf()]
        R_ss, B_ss = A2(64 * 4, "r_ss"); ss = R_ss.f32()
        R_on, B_on = A2(4096, "r_on"); on = R_on.bf().rearrange("p (c e) -> p c e", c=16)
        R_junk, B_junk = A2(128 * 2, "r_junk"); rjunk = R_junk.bf()
        R_mr, B_mr = A2(4096, "r_mix"); mr = R_mr.bf()

        def ret_A(h):
            s_ = h % 2
            rqT, B_q = RQ[s_][0].bf(), RQ[s_][1]
            rkT, B_k = RK[s_][0].bf(), RK[s_][1]
            rvT, B_v = RV[s_][0].bf(), RV[s_][1]
            rsg, B_sg = RSG[s_][0].bf(), RSG[s_][1]
            wq, wqb = wload(W4, w_in, 3072 + h * 128, 128)
            wk, wkb = wload(W4, w_in, 4096 + h * 128, 128)
            wv, wvb = wload(W4, w_in, 5120 + h * 128, 128)
            wg, wgb = wload(W4, w_in, 6144 + h * 128, 128)
            out = []

            def mk(wap, wbuf, tg, evac):
                stt = {}

                def piece(bs, q4):
                    if q4 == 0:
                        stt["b"] = psum(bs)
                    bi, bk, bb = stt["b"]
                    for c in range(q4 * 4, q4 * 4 + 4):
                        MM(bk, wap[:, c, 0:128], nT[:, c, tg * 512:(tg + 1) * 512], c == 0, c == KC - 1,
                           [wbuf, B_nT[tg]], [bb], c == KC - 1)
                    if q4 == 3:
                        evac(tg, bk, bb)
                for q4 in range(4):
                    out.append(lambda bs, q4=q4: piece(bs, q4))

            def evg(tg, bk, bb):
                ACT(re_, bk, AF.Exp, [], [bb, B_e], scale=-1.0)
                ACT(re_, re_, AF.Ln, [B_eps], [B_e], bias=oneb)
                ACT(re_, re_, AF.Exp, [], [B_e], scale=-1.0)
                TT("dve", rsg[:, tg * 512:(tg + 1) * 512], bk, re_, ALU.mult, [B_e], [bb, B_sg])
            for tg in range(4):
                mk(wk, wkb, tg, lambda tg_, bk, bb: CP("dve", rkT[:, tg_ * 512:(tg_ + 1) * 512], bk, [], [bb, B_k]))
            for tg in range(4):
                mk(wv, wvb, tg, lambda tg_, bk, bb: CP("act", rvT[:, tg_ * 512:(tg_ + 1) * 512], bk, [], [bb, B_v]))
            for tg in range(4):
                mk(wq, wqb, tg, lambda tg_, bk, bb: ACT(rqT[:, tg_ * 512:(tg_ + 1) * 512], bk, AF.Copy, [], [bb, B_q], scale=float(128 ** -0.5)))
            for tg in range(4):
                mk(wg, wgb, tg, evg)
            return out

        fillq = {"q": []}

        def fill(k):
            for _ in range(k):
                if fillq["q"]:
                    fillq["q"].pop(0)((0, 1))

        def ret_B1(h):
            s_ = h % 2
            rqT, B_q = RQ[s_][0].bf(), RQ[s_][1]
            rkT, B_k = RK[s_][0].bf(), RK[s_][1]
            rvT, B_v = RV[s_][0].bf(), RV[s_][1]
            for half in range(2):
                bi, bk, bb = psum([6, 7])
                bkb = bk.bitcast(BF16)
                for c in range(8):
                    cc = half * 8 + c
                    TR(bkb[:, c * 128:(c + 1) * 128], rkT[:, cc * 128:(cc + 1) * 128], identb, [B_k, B_idb], [bb], c == 7)
                src3 = bkb.rearrange("p (c t) -> p c t", c=8)
                TS("dve", kdLt[:, half * 8:half * 8 + 8, :], src3, kdL[:, h:h + 1], None, ALU.mult, None, [B_sm], [bb, B_kl])
                ACT(kdRt[:, half * 8:half * 8 + 8, :], src3, AF.Copy, [B_sm], [bb, B_kr], scale=kdR[:, h:h + 1])
                fill(2)
                bi, bk, bb = psum([6, 7])
                bkb = bk.bitcast(BF16)
                for c in range(8):
                    cc = half * 8 + c
                    TR(bkb[:, c * 128:(c + 1) * 128], rvT[:, cc * 128:(cc + 1) * 128], identb, [B_v, B_idb], [bb], c == 7)
                CP(alt(), vtm[:, half * 8:half * 8 + 8, :], bkb.rearrange("p (c t) -> p c t", c=8), [], [bb, B_vt])
                fill(2)
            for a, (kt_, kb_) in enumerate(((kdLt, B_kl), (kdRt, B_kr))):
                for g4 in range(4):
                    bi, bk, bb = psum([4, 5])
                    for c4 in range(4):
                        c = g4 * 4 + c4
                        MM(bk[:, c4 * 128:(c4 + 1) * 128], kt_[:, c, :], vtm[:, c, :], True, True, [kb_, B_vt], [bb], c4 == 3)
                    CP("act", Ut[:, a, g4 * 4:g4 * 4 + 4, :], bk.rearrange("p (c e) -> p c e", c=4), [], [bb, B_U])
                    fill(1)
            q3 = rqT.rearrange("p (c i) -> p c i", c=16)
            TT("dve", qdL.rearrange("p (c i) -> p c i", c=16), q3, decq[:, 0, h, :].unsqueeze(1).broadcast_to([128, 16, 128]), ALU.mult, [B_q, B_dq], [B_ql])
            TT("dve", qdR.rearrange("p (c i) -> p c i", c=16), q3, decq[:, 1, h, :].unsqueeze(1).broadcast_to([128, 16, 128]), ALU.mult, [B_q, B_dq], [B_qr])
            MS("dve", cur[:, 0, :], 0.0, [B_curs[0]])
            CP("dve", cur[:, 1, :], SRb[:, h, :], [B_srb[h]], [B_curs[1]])
            for c in range(16):
                CP("dve", Sbf[:, 0, c, :], cur[:, 0, :], [B_curs[0]], [B_Sbs[0]])
                STT("dve", cur[:, 0, :], cur[:, 0, :], gCL[:, h:h + 1], Ut[:, 0, c, :], ALU.mult, ALU.add, [B_U, B_sm], [B_curs[0]])
                c2 = 15 - c
                CP("dve", Sbf[:, 1, c2, :], cur[:, 1, :], [B_curs[1]], [B_Sbs[1]])
                STT("dve", cur[:, 1, :], cur[:, 1, :], gCR[:, h:h + 1], Ut[:, 1, c2, :], ALU.mult, ALU.add, [B_U, B_sm], [B_curs[1]])
            fill(16)

        def ret_B2(h):
            s_ = h % 2
            rqT, B_q = RQ[s_][0].bf(), RQ[s_][1]
            rkT, B_k = RK[s_][0].bf(), RK[s_][1]
            rsg, B_sg = RSG[s_][0].bf(), RSG[s_][1]
            MS("dve", ss[:, 0:16], 0.0, [B_ss])
            for g4 in range(4):
                bi, bk, bb = psum([2, 3])
                for c4 in range(4):
                    c = g4 * 4 + c4
                    MM(bk[:, c4 * 128:(c4 + 1) * 128], rkT[:, c * 128:(c + 1) * 128], rqT[:, c * 128:(c + 1) * 128], True, True, [B_k, B_q], [bb], c4 == 3)
                pa = g4 % 2
                TT("dve", PTt[:, pa, :], bk, MT[:, h, :], ALU.mult, [B_MT], [bb, B_ptl[pa]])
                fill(3)
                oi = [4, 5][g4 % 2]
                ob, obb = banks[oi][:], PB[oi]
                for c4 in range(4):
                    c = g4 * 4 + c4
                    osl = ob[:, c4 * 128:(c4 + 1) * 128]
                    MM(osl, PTt[:, pa, c4 * 128:(c4 + 1) * 128], vtm[:, c, :], True, False, [B_ptl[pa], B_vt], [obb], False)
                    MM(osl, qdL[:, c * 128:(c + 1) * 128], Sbf[:, 0, c, :], False, False, [B_ql, B_Sbs[0]], [obb], False)
                    MM(osl, qdR[:, c * 128:(c + 1) * 128], Sbf[:, 1, c, :], False, True, [B_qr, B_Sbs[1]], [obb], c4 == 3)
                for c4 in range(4):
                    c = g4 * 4 + c4
                    ACT(rjunk, ob[:, c4 * 128:(c4 + 1) * 128], AF.Square, [], [obb, B_junk, B_ss], scale=float(128 ** -0.5), accum=ss[:, c:c + 1])
                ACT(ss[:, 16 + g4 * 4:20 + g4 * 4], ss[:, g4 * 4:g4 * 4 + 4], AF.Ln, [B_eps], [B_ss], bias=epsb)
                ACT(ss[:, 32 + g4 * 4:36 + g4 * 4], ss[:, 16 + g4 * 4:20 + g4 * 4], AF.Exp, [], [B_ss], scale=-0.5)
                for c4 in range(4):
                    c = g4 * 4 + c4
                    ACT(on[:, c, :], ob[:, c4 * 128:(c4 + 1) * 128], AF.Copy, [B_ss], [obb, B_on], scale=ss[:, 32 + c:33 + c])
                fill(3)
            for half in range(2):
                bi, bk, bb = psum([6, 7])
                bkb = bk.bitcast(BF16)
                for c in range(8):
                    cc = half * 8 + c
                    TR(bkb[:, c * 128:(c + 1) * 128], on[:, cc, :], identb, [B_on, B_idb], [bb], c == 7)
                STT("dve", mr[:, half * 1024:(half + 1) * 1024], bkb, pp[:, 32 + h:33 + h], rsg[:, half * 1024:(half + 1) * 1024], ALU.mult, ALU.mult, [B_pp, B_sg], [bb, B_mr])
                fill(2)
            spill(8 + h, mr, B_mr, 0, 2048, "mr")
            fill(100)

        for p_ in ret_A(0):
            p_((0, 1, 2, 3))
        for h in range(NH):
            fillq["q"] = ret_A(h + 1) if h + 1 < NH else []
            ret_B1(h)
            ret_B2(h)
        for r in [x_[0] for x_ in RQ + RK + RV + RSG] + [R_e, R_ql, R_qr, R_kl, R_kr, R_vt, R_U, R_Sb, R_cur, R_pt, R_ss, R_on, R_junk, R_mr]:
            AR.free(r)

    for r in W4["regs"]:
        AR.free(r)
    for r in (R_MT, R_dq, R_nT):
        AR.free(r)
    AR.free(R_srb)
    R_hT = AR.alloc(KC * 1024 * 4, "hT"); hT = R_hT.f32().rearrange("p (c t) -> p c t", c=KC)
    B_hT = [R_hT.buf("hT%d" % tg) for tg in range(2)]
    R_32 = AR.alloc(KC * 1024 * 2, "r32"); r32 = R_32.bf().rearrange("p (c t) -> p c t", c=KC)
    B_32 = [R_32.buf("r32_%d" % tg) for tg in range(2)]
    R_rbc = AR.alloc(1024 * 4, "rbc"); rbc = R_rbc.f32(); B_rbc = [R_rbc.buf("rbc%d" % tg) for tg in range(2)]
    R_xy = [AR.alloc(D * 4, "xy%d" % i) for i in range(2)]
    xy = [r.f32() for r in R_xy]; B_xy = [r.buf() for r in R_xy]
    R_at = [AR.alloc(4 * 1024 * 2, "actT%d" % i) for i in range(2)]
    actT = [r.bf().rearrange("p (j t) -> p j t", j=4) for r in R_at]; B_at = [r.buf() for r in R_at]
    R_sgt = [AR.alloc(512 * 2, "sgt%d" % i) for i in range(2)]
    sgt = [r.bf() for r in R_sgt]; B_sgt = [r.buf() for r in R_sgt]
    W8b = make_wslots(4, 256, "w8b_")
    R_wd = [AR.alloc(4 * D * 2, "wd%d" % i) for i in range(2)]
    wd = [r.bf().rearrange("p (j n) -> p j n", j=4) for r in R_wd]; B_wd = [r.buf() for r in R_wd]
    wd_n = {"n": 0}
    B_y = [Buf("y%d" % i) for i in range(16)]
    mix_v = mix_d.rearrange("h p t -> p h t")
    sgn = {"n": 0}

    def feat_norm(nw_col0, final):
        for tg in range(2):
            if final:
                sqv = R_wd[tg].bf().rearrange("p (c t) -> p c t", c=KC)
                sqb = B_wd[tg]
            else:
                sqv = r32[:, :, tg * 512:(tg + 1) * 512]
                sqb = B_32[tg]
            ACT(sqv, hT[:, :, tg * 512:(tg + 1) * 512], AF.Square, [B_hT[tg]], [sqb])
            bi, bk, bb = psum([6, 7])
            for c in range(KC):
                MM(bk, onesb, sqv[:, c, :], c == 0, c == KC - 1, [B_ones, sqb], [bb], c == KC - 1)
            ACT(rbc[:, tg * 512:(tg + 1) * 512], bk, AF.Ln, [B_eps], [bb, B_rbc[tg]], scale=float(1.0 / D), bias=epsb)
            ACT(rbc[:, tg * 512:(tg + 1) * 512], rbc[:, tg * 512:(tg + 1) * 512], AF.Exp, [], [B_rbc[tg]], scale=-0.5)
        for tg in range(2):
            for c in range(KC):
                if final:
                    STT("dve", hT[:, c, tg * 512:(tg + 1) * 512], hT[:, c, tg * 512:(tg + 1) * 512], pp[:, nw_col0 + c:nw_col0 + c + 1],
                        rbc[:, tg * 512:(tg + 1) * 512], ALU.mult, ALU.mult, [B_pp, B_rbc[tg]], [B_hT[tg]])
                else:
                    STT("dve", r32[:, c, tg * 512:(tg + 1) * 512], hT[:, c, tg * 512:(tg + 1) * 512], pp[:, nw_col0 + c:nw_col0 + c + 1],
                        rbc[:, tg * 512:(tg + 1) * 512], ALU.mult, ALU.mult, [B_pp, B_rbc[tg], B_hT[tg]], [B_32[tg]])

    dbank = {"n": 0}

    def down_group(g, aslot, wslot):
        pairs = [(cb, tg) for cb in range(KC) for tg in range(2)]
        for pi in range(0, len(pairs), 2):
            bsel = []
            for _ in range(2):
                bsel.append([4, 5, 6, 7][dbank["n"] % 4])
                dbank["n"] += 1
            S.prewait("pe", [PB[b_] for b_ in bsel])
            for (cb, tg), b_ in zip(pairs[pi:pi + 2], bsel):
                bk, bb = banks[b_][:], PB[b_]
                for jj in range(4):
                    MM(bk, wd[wslot][:, jj, cb * 128:(cb + 1) * 128], actT[aslot][:, jj, tg * 512:(tg + 1) * 512], jj == 0, jj == 3,
                       [B_wd[wslot], B_at[aslot]], [bb], jj == 3)
                TT("dve", hT[:, cb, tg * 512:(tg + 1) * 512], hT[:, cb, tg * 512:(tg + 1) * 512], bk, ALU.add, [], [bb, B_hT[tg]])

    def prologue_dma(tt):
        tok0 = tt * 1024
        all_mix = [B_mixd[i][tt] for i in range(16)]
        S.dma("sp", r32, mix_v[:, :, tok0:tok0 + 1024], "r32", reads=all_mix, writes=B_32)
        for s in range(2):
            S.dma("sp", xy[s], x[tok0 + s * 128: tok0 + (s + 1) * 128, :], "xy%d" % s, writes=[B_xy[s]])

    prologue_dma(0)
    for tt in range(2):
        tok0 = tt * 1024
        for s in range(8):
            sl = s % 2
            if s >= 2:
                S.dma("sp", xy[sl], x[tok0 + s * 128: tok0 + (s + 1) * 128, :], "xy%d" % sl, writes=[B_xy[sl]])
            for g4 in range(4):
                bi, bk, bb = psum([6, 7])
                for c4 in range(4):
                    cc = g4 * 4 + c4
                    TR(bk[:, c4 * 128:(c4 + 1) * 128], xy[sl][:, cc * 128:(cc + 1) * 128], identf, [B_xy[sl], B_idf], [bb], c4 == 3)
                CP(alt(), hT[:, g4 * 4:g4 * 4 + 4, s * 128:(s + 1) * 128], bk.rearrange("p (c t) -> p c t", c=4), [], [bb, B_hT[s // 4]])
        for u in range(8):
            wap, wbuf = wload(W8b, w_out, u * 256, 256)
            for cb2 in range(2):
                for tg in range(2):
                    bi, bk, bb = psum([0, 1, 2, 3])
                    for c in range(KC):
                        MM(bk, wap[:, c, cb2 * 128:(cb2 + 1) * 128], r32[:, c, tg * 512:(tg + 1) * 512], c == 0, c == KC - 1,
                           [wbuf, B_32[tg]], [bb], c == KC - 1)
                    cb = u * 2 + cb2
                    TT("dve", hT[:, cb, tg * 512:(tg + 1) * 512], hT[:, cb, tg * 512:(tg + 1) * 512], bk, ALU.add, [], [bb, B_hT[tg]])
        if "h1" in dbg and tt == 0:
            pass
        feat_norm(0, False)
        prev = None
        for g in range(11):
            aslot = g % 2
            wslot = wd_n["n"] % 2
            wd_n["n"] += 1
            units = []
            for a in range(2):
                col0 = (g * 4 + a * 2) * 128
                units.append((wload(W8b, w_gate, col0, 256), wload(W8b, w_up, col0, 256)))
            srcd = w_down[g * 512:(g + 1) * 512, :].rearrange("(j p) n -> p j n", p=128)
            S.dma("pool", wd[wslot], srcd, "wd%d" % wslot, writes=[B_wd[wslot]])
            for jj in range(4):
                (wg_ap, wg_b), (wu_ap, wu_b) = units[jj // 2]
                cb2 = jj % 2
                for tg in range(2):
                    gi, gk, gb = psum([0, 1, 2, 3])
                    ui, uk, ub = psum([0, 1, 2, 3])
                    S.prewait("pe", [gb, ub])
                    for c in range(KC):
                        MM(gk, wg_ap[:, c, cb2 * 128:(cb2 + 1) * 128], r32[:, c, tg * 512:(tg + 1) * 512], c == 0, c == KC - 1,
                           [wg_b, B_32[tg]], [gb], c == KC - 1)
                    for c in range(KC):
                        MM(uk, wu_ap[:, c, cb2 * 128:(cb2 + 1) * 128], r32[:, c, tg * 512:(tg + 1) * 512], c == 0, c == KC - 1,
                           [wu_b, B_32[tg]], [ub], c == KC - 1)
                    ss_ = sgn["n"] % 2
                    sgn["n"] += 1
                    ACT(sgt[ss_], gk, AF.Silu, [], [gb, B_sgt[ss_]])
                    TT("dve", actT[aslot][:, jj, tg * 512:(tg + 1) * 512], uk, sgt[ss_], ALU.mult, [B_sgt[ss_]], [ub, B_at[aslot]])
            if prev is not None:
                down_group(*prev)
            prev = (g, aslot, wslot)
        down_group(*prev)
        if tt + 1 < 2:
            prologue_dma(tt + 1)
        feat_norm(16, True)
        for s in range(8):
            sl = s % 2
            yst = R_at[sl].f32()
            for g4 in range(4):
                bi, bk, bb = psum([6, 7])
                for c4 in range(4):
                    cc = g4 * 4 + c4
                    TR(bk[:, c4 * 128:(c4 + 1) * 128], hT[:, cc, s * 128:(s + 1) * 128], identf, [B_hT[s // 4], B_idf], [bb], c4 == 3)
                CP(alt(), yst[:, g4 * 512:(g4 + 1) * 512], bk, [], [bb, B_at[sl]])
            S.dma("sp", y[tok0 + s * 128: tok0 + (s + 1) * 128, :], yst, "yst%d" % sl, reads=[B_at[sl]], writes=[B_y[tt * 8 + s]])
    S.wait_all("sp", B_y)
    if "mixT" in dbg:
        S.wait_all("sp", [b for bb_ in B_mixd for b in bb_])
    print("arena peak (KiB):", AR.peak * 2 / 1024.0, "insts:", S.n_inst)
    S.emit()
    return nc


def make_in_maps(inputs, cores=range(8)):
    f = lambda k: np.ascontiguousarray(np.asarray(inputs[k], dtype=np.float32))
    x = f("x")
    w_in, w_out, w_gate, w_up, w_down = f("w_in")[0], f("w_out")[0], f("w_gate")[0], f("w_up")[0], f("w_down")[0]
    nmix, nffn, nfin = f("norm_mix_w")[0], f("norm_ffn_w")[0], f("norm_final_w")
    retw = f("ret_norm_w")[0]
    dfw, dbw = f("ret_decay_fwd")[0], f("ret_decay_bwd")[0]
    cst = host_consts()
    idxm = np.arange(2048, dtype=np.float32)
    pp = np.concatenate([nffn.reshape(16, 128).T, nfin.reshape(16, 128).T, retw.reshape(8, 128).T], axis=1)
    pp = np.ascontiguousarray(pp, dtype=np.float32)
    maps = []
    for c in cores:
        b, hf = c // 2, c % 2
        xb = x[b] if hf == 0 else x[b][::-1]
        dL, dR = (dfw, dbw) if hf == 0 else (dbw, dfw)
        bcv = np.concatenate([nmix, dL, dR]).astype(np.float32)
        maps.append({"x": np.ascontiguousarray(xb), "w_in": w_in, "w_out": w_out, "w_gate": w_gate, "w_up": w_up,
                     "w_down": w_down, "cst": cst, "pp": pp, "bcv": bcv, "idxm": idxm})
    return maps


def kernel(**inputs):
    nc = build_program()
    maps = make_in_maps(inputs)
    res = run_bass_kernel_spmd(nc, maps, core_ids=list(range(8)))
    out = np.empty((4, 4096, 2048), np.float32)
    for c in range(8):
        b, hf = c // 2, c % 2
        yc = np.asarray(res.results[c]["y"], dtype=np.float32)
        if hf == 0:
            out[b, :2048] = yc
        else:
            out[b, 2048:] = yc[::-1]
    return out
```

```python
import numpy as np
import ml_dtypes
import concourse.bass as bass
import concourse.mybir as mybir
from concourse.bass_utils import run_bass_kernel_spmd

F32 = mybir.dt.float32
BF16 = mybir.dt.bfloat16
AF = mybir.ActivationFunctionType
ALU = mybir.AluOpType
AX = mybir.AxisListType

ENGS = ("pe", "act", "dve", "pool", "sp")


class Buf:
    def __init__(self, name):
        self.name = name
        self.w = None
        self.r = {}


class Sched:
    def __init__(self, nc):
        self.nc = nc
        self.ops = {e: [] for e in ENGS}
        self.cnt = {e: 0 for e in ENGS}
        self.known = {e: {} for e in ENGS}
        self.dma_cnt = {}
        self.n_inst = {e: 0 for e in ENGS}

    def _deps(self, eng, reads, writes):
        deps = []
        for b in reads:
            if b.w is not None:
                deps.append(b.w)
        for b in writes:
            if b.w is not None:
                deps.append(b.w)
            deps.extend(b.r.items())
        need = {}
        for key, val in deps:
            if key == "pe" and eng == "pe":
                continue
            if self.known[eng].get(key, 0) >= val:
                continue
            if need.get(key, 0) < val:
                need[key] = val
        for key, val in need.items():
            self.known[eng][key] = val
            self.ops[eng].append(("wait", key, val))

    def _record(self, ev, reads, writes):
        key, val = ev
        for b in reads:
            if b.r.get(key, 0) < val:
                b.r[key] = val
        for b in writes:
            b.w = ev
            b.r = {}

    def op(self, eng, fn, reads=(), writes=(), signal=True):
        self._deps(eng, reads, writes)
        if signal:
            self.cnt[eng] += 1
            ev = (eng, self.cnt[eng])
        else:
            ev = (eng, self.cnt[eng] + 1)
        self.ops[eng].append(("op", fn, signal))
        self.n_inst[eng] += 1
        self._record(ev, reads, writes)

    def dma(self, eng, out_ap, in_ap, key, reads=(), writes=()):
        self._deps(eng, reads, writes)
        k = ("dma", key)
        self.dma_cnt[k] = self.dma_cnt.get(k, 0) + 16
        ev = (k, self.dma_cnt[k])
        self.ops[eng].append(("dma", out_ap, in_ap, k))
        self.n_inst[eng] += 1
        self._record(ev, reads, writes)

    def prewait(self, eng, bufs):
        self._deps(eng, [], bufs)

    def wait_all(self, eng, bufs):
        self._deps(eng, [], bufs)

    def emit(self):
        nc = self.nc
        from contextlib import ExitStack
        with ExitStack() as st:
            sems = {}
            for e in ("pe", "act", "dve", "pool"):
                sems[e] = st.enter_context(nc.semaphore("s_" + e))
            for i, k in enumerate(sorted(self.dma_cnt.keys(), key=str)):
                sems[k] = st.enter_context(nc.semaphore("d%d" % i))
            block = st.enter_context(nc.Block())

            def run(eng_name, eng):
                for o in self.ops[eng_name]:
                    if o[0] == "wait":
                        eng.wait_ge(sems[o[1]], o[2])
                    elif o[0] == "op":
                        ins = o[1](eng)
                        if o[2]:
                            ins.then_inc(sems[eng_name], 1)
                    else:
                        eng.dma_start(out=o[1], in_=o[2]).then_inc(sems[o[3]], 16)

            @block.tensor
            def _(eng):
                run("pe", eng)

            @block.scalar
            def _(eng):
                run("act", eng)

            @block.vector
            def _(eng):
                run("dve", eng)

            @block.gpsimd
            def _(eng):
                run("pool", eng)

            @block.sync
            def _(eng):
                run("sp", eng)


class Region:
    def __init__(self, arena, off, size, name, prior, nbytes):
        self.arena, self.off, self.size, self.name = arena, off, size, name
        self.n = nbytes // 2
        self.prior = prior
        self.bufs = []

    def buf(self, name=None):
        b = Buf(name or self.name)
        b.r = dict(self.prior)
        self.bufs.append(b)
        return b

    def bf(self):
        return self.arena.t[:, self.off:self.off + self.n]

    def f32(self):
        return self.arena.t[:, self.off:self.off + self.n].bitcast(F32)


class Arena:
    def __init__(self, tensor, total):
        self.t, self.total = tensor, total
        self.live = []
        self.dead = []
        self.peak = 0

    def alloc(self, nbytes, name):
        size = ((nbytes + 63) // 64) * 32
        self.live.sort(key=lambda r: r.off)
        off = 0
        for r in self.live:
            if r.off - off >= size:
                break
            off = max(off, r.off + r.size)
        if off + size > self.total:
            raise RuntimeError("arena OOM for %s (%d B); live=%s" % (
                name, nbytes, [(r.name, r.size * 2) for r in self.live]))
        prior = {}
        for (o, sz, ev) in self.dead:
            if o < off + size and off < o + sz:
                for k, v in ev.items():
                    if prior.get(k, 0) < v:
                        prior[k] = v
        reg = Region(self, off, size, name, prior, nbytes)
        self.live.append(reg)
        self.peak = max(self.peak, off + size)
        return reg

    def free(self, reg):
        self.live.remove(reg)
        ev = dict(reg.prior)
        for b in reg.bufs:
            items = list(b.r.items())
            if b.w is not None:
                items.append(b.w)
            for k, v in items:
                if ev.get(k, 0) < v:
                    ev[k] = v
        self.dead.append((reg.off, reg.size, ev))


D = 2048
T_OWN = 2048
KC = 16
DFF = 5632
NH = 8
EPS = 1e-6
C_ID, C_DPOS, C_DNEG, C_IQ1, C_IQR, C_PL, C_PR, C_ABSD, C_VALID, NCST = 0, 128, 256, 384, 512, 640, 641, 642, 898, 1154
MERGE_DEN = True
NXT = 4
ARENA_ELEMS = 105472


def host_consts():
    c = np.zeros((128, NCST), np.float32)
    p = np.arange(128)[:, None].astype(np.float32)
    i = np.arange(128)[None, :].astype(np.float32)
    c[:, C_ID:C_ID + 128] = np.eye(128, dtype=np.float32)
    c[:, C_DPOS:C_DPOS + 128] = np.maximum(i - p, 0)
    c[:, C_DNEG:C_DNEG + 128] = np.maximum(p - i, 0)
    c[:, C_IQ1:C_IQ1 + 128] = i + 1
    c[:, C_IQR:C_IQR + 128] = 128 - i
    c[:, C_PL] = 127 - p[:, 0]
    c[:, C_PR] = p[:, 0]
    cc = np.arange(256)[None, :].astype(np.float32)
    c[:, C_ABSD:C_ABSD + 256] = np.abs(cc - 64 - p)
    c[:, C_VALID:C_VALID + 256] = ((cc >= p) & (cc <= p + 128)).astype(np.float32)
    return c


def build_program(dbg=()):
    nc = bass.Bass("TRN2", target_bir_lowering=False)
    dt = nc.dram_tensor
    x = dt("x", [4096, D], F32, kind="ExternalInput").ap()
    w_in = dt("w_in", [D, 7168], F32, kind="ExternalInput").ap()
    w_out = dt("w_out", [D, D], F32, kind="ExternalInput").ap()
    w_gate = dt("w_gate", [D, DFF], F32, kind="ExternalInput").ap()
    w_up = dt("w_up", [D, DFF], F32, kind="ExternalInput").ap()
    w_down = dt("w_down", [DFF, D], F32, kind="ExternalInput").ap()
    cst_d = dt("cst", [128, NCST], F32, kind="ExternalInput").ap()
    pp_d = dt("pp", [128, 40], F32, kind="ExternalInput").ap()
    bcv_d = dt("bcv", [D + 16], F32, kind="ExternalInput").ap()
    idxm_d = dt("idxm", [2048], F32, kind="ExternalInput").ap()
    y = dt("y", [T_OWN, D], F32, kind="ExternalOutput").ap()
    mix_d = dt("mixT", [16, 128, T_OWN], BF16, kind=("ExternalOutput" if "mixT" in dbg else "Internal")).ap()
    dbg_out = {}

    S = Sched(nc)
    arena_t = nc.alloc_sbuf_tensor("arena", [128, ARENA_ELEMS], BF16)
    AR = Arena(arena_t, ARENA_ELEMS)
    banks = [nc.alloc_psum_tensor("bank%d" % i, [128, 512], F32) for i in range(8)]
    PB = [Buf("bank%d" % i) for i in range(8)]
    bank_rr = {"n": 0}

    def psum(which=None):
        if which is None:
            which = range(8)
        which = list(which)
        i = which[bank_rr["n"] % len(which)]
        bank_rr["n"] += 1
        return i, banks[i][:], PB[i]

    def MM(out, lhsT, rhs, start, stop, reads, writes, signal, skip=False):
        S.op("pe", lambda e: e.matmul(out, lhsT=lhsT, rhs=rhs, start=start, stop=stop, skip_group_check=skip),
             reads, writes, signal)

    def TR(out, in_, ident, reads, writes, signal):
        S.op("pe", lambda e: e.transpose(out, in_, ident), reads, writes, signal)

    def ACT(out, in_, func, reads, writes, scale=1.0, bias=None, accum=None):
        kw = {}
        if bias is not None:
            kw["bias"] = bias
        if accum is not None:
            kw["accum_out"] = accum
        S.op("act", lambda e: e.activation(out, in_, func, scale=scale, **kw), reads, writes)

    def CP(eng, out, in_, reads, writes):
        if eng == "act":
            S.op("act", lambda e: e.activation(out, in_, AF.Copy), reads, writes)
        else:
            S.op(eng, lambda e: e.tensor_copy(out, in_), reads, writes)

    def TT(eng, out, a, b, op, reads, writes):
        S.op(eng, lambda e: e.tensor_tensor(out, a, b, op=op), reads, writes)

    def STT(eng, out, in0, scalar, in1, op0, op1, reads, writes):
        S.op(eng, lambda e: e.scalar_tensor_tensor(out, in0, scalar, in1, op0=op0, op1=op1), reads, writes)

    def TS(eng, out, in0, s1, s2, op0, op1, reads, writes):
        if s2 is None:
            S.op(eng, lambda e: e.tensor_scalar(out, in0, s1, None, op0=op0), reads, writes)
        else:
            S.op(eng, lambda e: e.tensor_scalar(out, in0, s1, s2, op0=op0, op1=op1), reads, writes)

    def MS(eng, out, val, writes):
        S.op(eng, lambda e: e.memset(out, val), (), writes)

    def RECIP(out, in_, reads, writes):
        S.op("dve", lambda e: e.reciprocal(out, in_), reads, writes)

    rr = {"n": 0}

    def alt():
        rr["n"] += 1
        return "act" if rr["n"] % 2 else "dve"

    R_idf = AR.alloc(128 * 4, "identf"); identf = R_idf.f32(); B_idf = R_idf.buf()
    R_idb = AR.alloc(128 * 2, "identb"); identb = R_idb.bf(); B_idb = R_idb.buf()
    R_ones = AR.alloc(128 * 2, "ones"); onesb = R_ones.bf(); B_ones = R_ones.buf()
    R_pp = AR.alloc(40 * 4, "pp"); pp = R_pp.f32(); B_pp = R_pp.buf()
    R_eps = AR.alloc(64, "eps"); epsb = R_eps.f32()[:, 0:1]; oneb = R_eps.f32()[:, 1:2]; B_eps = R_eps.buf()
    R_sm = AR.alloc(64 * 4, "small"); sm = R_sm.f32(); B_sm = R_sm.buf()
    lgL, lgR, kdL, kdR, gCL, gCR = (sm[:, 8 * i:8 * i + 8] for i in range(6))
    R_srb = AR.alloc(NH * 128 * 4, "SRb"); SRb = R_srb.f32().rearrange("p (h e) -> p h e", h=NH); B_srb = [R_srb.buf("SRb%d" % h) for h in range(NH)]

    R_bcv = AR.alloc((D + 16) * 4, "bcv"); bcv = R_bcv.f32(); B_bcv = R_bcv.buf()
    R_E = AR.alloc(24 * 256 * 2, "alibiE"); Emask = R_E.bf().rearrange("p (m c) -> p m c", m=24); B_E = R_E.buf()
    R_halo = AR.alloc(2 * NH * 1024 * 2, "halo"); halo = R_halo.bf().rearrange("p (a h t) -> p a h t", a=2, h=NH)
    B_halo = [[R_halo.buf("halo%d_%d" % (a, h)) for h in range(NH)] for a in range(2)]
    R_nT = AR.alloc(KC * 2048 * 2, "nT"); nT = R_nT.bf().rearrange("p (c t) -> p c t", c=KC)
    B_nT = [R_nT.buf("nT%d" % g) for g in range(4)]

    R_cst = AR.alloc(NCST * 4, "cst"); cst = R_cst.f32(); B_cst = R_cst.buf()
    S.dma("sp", cst, cst_d, "cst", writes=[B_cst])
    S.dma("sp", pp, pp_d, "pp", writes=[B_pp])
    S.dma("sp", bcv, bcv_d.partition_broadcast(128), "bcv", writes=[B_bcv])
    CP("dve", identf, cst[:, C_ID:C_ID + 128], [B_cst], [B_idf])
    CP("dve", identb, cst[:, C_ID:C_ID + 128], [B_cst], [B_idb])
    MS("dve", onesb, 1.0, [B_ones])
    MS("dve", epsb, EPS, [B_eps])
    MS("dve", oneb, 1.0, [B_eps])
    ACT(sm[:, 0:16], bcv[:, D:D + 16], AF.Exp, [B_bcv], [B_sm])
    TS("dve", sm[:, 0:16], sm[:, 0:16], -1.0, None, ALU.mult, None, [B_sm], [B_sm])
    ACT(kdL, lgL, AF.Exp, [B_sm, B_cst], [B_sm], scale=cst[:, C_PL:C_PL + 1])
    ACT(kdR, lgR, AF.Exp, [B_sm, B_cst], [B_sm], scale=cst[:, C_PR:C_PR + 1])
    ACT(sm[:, 32:48], sm[:, 0:16], AF.Exp, [B_sm], [B_sm], scale=128.0)
    R_tmp = AR.alloc(512 * 4, "setup_tmp"); tmpf = R_tmp.f32(); B_tmp = R_tmp.buf()
    for d_i, dil in enumerate((1, 4, 16)):
        for h in range(NH):
            slope = 2.0 ** (-(h + 1))
            ACT(tmpf[:, 0:256], cst[:, C_ABSD:C_ABSD + 256], AF.Exp, [B_cst], [B_tmp], scale=-slope * dil)
            TT("dve", Emask[:, d_i * NH + h, :], tmpf[:, 0:256], cst[:, C_VALID:C_VALID + 256], ALU.mult, [B_tmp, B_cst], [B_E])
    AR.free(R_tmp)
    AR.free(R_cst)

    def norm_tokens(tok0):
        R_xt = [AR.alloc(D * 4, "xt%d" % i) for i in range(NXT)]
        R_nb = [AR.alloc(D * 2, "nb%d" % i) for i in range(3)]
        R_junk = AR.alloc(D * 2, "junk")
        R_sts = [AR.alloc(64, "nstat%d" % i) for i in range(2)]
        xt = [r.f32() for r in R_xt]; Bxt = [r.buf() for r in R_xt]
        nb = [r.bf() for r in R_nb]; Bnb = [r.buf() for r in R_nb]
        junk = R_junk.bf(); Bjunk = R_junk.buf()
        def stage_a(s):
            sl = s % 2
            st = R_sts[sl].f32(); Bst = R_sts[sl].bufs[0] if R_sts[sl].bufs else R_sts[sl].buf()
            xs = s % NXT
            ns_ = s % 3
            S.dma("sp", xt[xs], x[tok0 + s * 128: tok0 + (s + 1) * 128, :], "xt%d" % xs, writes=[Bxt[xs]])
            MS("dve", st[:, 0:1], 0.0, [Bst])
            ACT(junk, xt[xs], AF.Square, [Bxt[xs]], [Bjunk, Bst], scale=float(D ** -0.5), accum=st[:, 0:1])
            ACT(st[:, 1:2], st[:, 0:1], AF.Ln, [Bst, B_eps], [Bst], bias=epsb)
            ACT(st[:, 2:3], st[:, 1:2], AF.Exp, [Bst], [Bst], scale=-0.5)
            STT("dve", nb[ns_], xt[xs], st[:, 2:3], bcv[:, 0:D], ALU.mult, ALU.mult, [Bxt[xs], Bst, B_bcv], [Bnb[ns_]])

        def stage_b(s):
            ns_ = s % 3
            g = s // 4
            for half in range(2):
                bi, bk, bb = psum([6, 7])
                bkb = bk.bitcast(BF16)
                for c in range(8):
                    cc = half * 8 + c
                    TR(bkb[:, c * 128:(c + 1) * 128], nb[ns_][:, cc * 128:(cc + 1) * 128], identb, [Bnb[ns_], B_idb], [bb], c == 7)
                CP("act" if half == 0 else "dve", nT[:, half * 8:half * 8 + 8, s * 128:(s + 1) * 128], bkb.rearrange("p (c t) -> p c t", c=8), [], [bb, B_nT[g]])
        stage_a(0)
        for s in range(16):
            if s + 1 < 16:
                stage_a(s + 1)
            stage_b(s)
        for r in R_xt + R_nb + [R_junk] + R_sts:
            AR.free(r)

    def make_wslots(n, ncols, name):
        regs = [AR.alloc(KC * ncols * 2, "%s%d" % (name, i)) for i in range(n)]
        return {"regs": regs, "aps": [r.bf().rearrange("p (c n) -> p c n", c=KC) for r in regs],
                "bufs": [r.buf() for r in regs], "n": 0, "name": name}

    def wload(ws, w_ap, col0, ncols):
        i = ws["n"] % len(ws["regs"])
        ws["n"] += 1
        src = w_ap.rearrange("(c p) n -> p c n", p=128)[:, :, col0:col0 + ncols]
        S.dma("pool", ws["aps"][i][:, :, 0:ncols], src, "%s%d" % (ws["name"], i), writes=[ws["bufs"][i]])
        return ws["aps"][i], ws["bufs"][i]

    def proj_block(wap, wbuf, wc0, tgs, evac, banks_sel=(0, 1, 2, 3)):
        for tg in tgs:
            bi, bk, bb = psum(banks_sel)
            for c in range(KC):
                MM(bk, wap[:, c, wc0:wc0 + 128], nT[:, c, tg * 512:(tg + 1) * 512], c == 0, c == KC - 1,
                   [wbuf, B_nT[tg]], [bb], c == KC - 1)
            evac(tg, bk, bb)

    norm_tokens(2048)
    W8 = make_wslots(3, 256, "w8_")
    for a, cbase in ((0, 1024), (1, 2048)):
        for u in range(4):
            wap, wbuf = wload(W8, w_in, cbase + u * 256, 256)
            for hh in range(2):
                h = 2 * u + hh

                def ev(tg, bk, bb, a=a, h=h):
                    CP(alt(), halo[:, a, h, tg * 512:(tg + 1) * 512], bk, [], [bb, B_halo[a][h]])
                proj_block(wap, wbuf, hh * 128, (0, 1), ev)

    R_idx = AR.alloc(2048 * 4, "idxm"); idxm = R_idx.f32(); B_idx = R_idx.buf()
    S.dma("sp", idxm, idxm_d.partition_broadcast(128), "idxm", writes=[B_idx])
    R_dec = AR.alloc(2048 * 2, "decfull"); decf = R_dec.bf(); B_dec = R_dec.buf()
    R_kT = AR.alloc(2048 * 2, "pre_kT"); pkT = R_kT.bf(); B_pkT = R_kT.buf()
    R_vT = AR.alloc(2048 * 2, "pre_vT"); pvT = R_vT.bf(); B_pvT = R_vT.buf()
    R_ktm = AR.alloc(2048 * 2, "pre_ktm"); pktm = R_ktm.bf().rearrange("p (c d) -> p c d", c=16); B_pktm = R_ktm.buf()
    R_vtm = AR.alloc(2048 * 2, "pre_vtm"); pvtm = R_vtm.bf().rearrange("p (c d) -> p c d", c=16); B_pvtm = R_vtm.buf()
    for u in range(4):
        wk, wkb = wload(W8, w_in, 4096 + u * 256, 256)
        wv, wvb = wload(W8, w_in, 5120 + u * 256, 256)
        for hh in range(2):
            h = 2 * u + hh
            ACT(decf, idxm, AF.Exp, [B_idx, B_sm], [B_dec], scale=lgR[:, h:h + 1])

            def evk(tg, bk, bb):
                TT("dve", pkT[:, tg * 512:(tg + 1) * 512], bk, decf[:, tg * 512:(tg + 1) * 512], ALU.mult, [B_dec], [bb, B_pkT])

            def evv(tg, bk, bb):
                CP("act", pvT[:, tg * 512:(tg + 1) * 512], bk, [], [bb, B_pvT])
            proj_block(wk, wkb, hh * 128, range(4), evk)
            proj_block(wv, wvb, hh * 128, range(4), evv)
            for (srcT, Bsrc, dst, Bdst) in ((pkT, B_pkT, pktm, B_pktm), (pvT, B_pvT, pvtm, B_pvtm)):
                for half in range(2):
                    bi, bk, bb = psum([6, 7])
                    bkb = bk.bitcast(BF16)
                    for c in range(8):
                        cc = half * 8 + c
                        TR(bkb[:, c * 128:(c + 1) * 128], srcT[:, cc * 128:(cc + 1) * 128], identb, [Bsrc, B_idb], [bb], c == 7)
                    CP(alt(), dst[:, half * 8:half * 8 + 8, :], bkb.rearrange("p (c t) -> p c t", c=8), [], [bb, Bdst])
            bi, bk, bb = psum([4, 5])
            for c in range(16):
                MM(bk[:, 0:128], pktm[:, c, :], pvtm[:, c, :], c == 0, c == 15, [B_pktm, B_pvtm], [bb], c == 15)
            CP("dve", SRb[:, h, :], bk[:, 0:128], [], [bb, B_srb[h]])
    for r in (R_idx, R_dec, R_kT, R_vT, R_ktm, R_vtm):
        AR.free(r)

    for r in W8["regs"]:
        AR.free(r)
    norm_tokens(0)
    W4 = make_wslots(6, 128, "w4_")

    B_mixd = [[Buf("mixd%d_%d" % (i, j)) for j in range(2)] for i in range(16)]

    def spill(h16, src_ap, src_buf, col0, ncols, key):
        wr = [B_mixd[h16][col0 // 1024]] if ncols == 1024 else B_mixd[h16]
        S.dma("sp", mix_d[h16, :, col0:col0 + ncols], src_ap, key, reads=[src_buf], writes=wr)

    AR.free(R_bcv)
    if "skip_attn" not in dbg:
        R_QKV = [[AR.alloc(2048 * 2, "%sT%d" % (nm, i)) for nm in "QKV"] for i in range(2)]
        QKVap = [[r.bf() for r in rr_] for rr_ in R_QKV]
        QKVb = [[r.buf() for r in rr_] for rr_ in R_QKV]
        R_Vt = [AR.alloc(32 * 128 * 2, "Vtm%d" % i) for i in range(2)]
        Vt = [r.bf().rearrange("p (s e) -> p s e", s=32) for r in R_Vt]; B_Vt = [r.buf() for r in R_Vt]
        NP = 4
        R_P = [AR.alloc(2 * 512 * 2, "P%d" % i) for i in range(NP)]
        Pt = [r.bf() for r in R_P]; B_P = [r.buf() for r in R_P]
        R_rc = AR.alloc(1024 * 4, "recip"); rc = R_rc.f32(); B_rc = R_rc.buf()
        R_mo = [AR.alloc(1024 * 2, "mixo%d" % i) for i in range(2)]
        mo = [r.bf() for r in R_mo]; B_mo = [r.buf() for r in R_mo]
        pc = {"n": 0}
        LOOK = 2

        def att_front(st, h):
            QT, KT, VT = QKVap[h % 2]
            B_Q, B_K, B_V = QKVb[h % 2]
            dil, u, n0, n1 = st["dil"], st["u"], st["n0"], st["n1"]
            kp = min(128, st["Lk"] - 128 * u)
            n = n1 - n0
            c0 = n0 - (128 * u - 64)
            G = len(st["rs"])
            bi, bk, bb = psum([2, 3])
            for k_, r_ in enumerate(st["rs"]):
                if u < st["n_own"]:
                    p0 = dil * 128 * u + r_
                    kap = KT[:, p0:p0 + dil * (kp - 1) + 1:dil]; kb_ = B_K
                else:
                    p0 = dil * 128 * u + r_ - 2048
                    kap = halo[:, 0, h, p0:p0 + dil * (kp - 1) + 1:dil]; kb_ = B_halo[0][h]
                q0 = dil * n0 + r_
                qap = QT[:, q0:q0 + dil * (n - 1) + 1:dil]
                MM(bk[0:kp, k_ * n:(k_ + 1) * n], kap, qap, True, True, [kb_, B_Q], [bb], k_ == G - 1)
            ps = pc["n"] % NP
            pc["n"] += 1
            praw = Pt[ps][0:kp, 0:G * n]
            pm = Pt[ps][0:kp, 512:512 + G * n]
            ACT(praw, bk[0:kp, 0:G * n], AF.Exp, [], [bb, B_P[ps]], scale=float(128 ** -0.5))
            em = st["em"]
            if G == 1:
                TT("dve", pm, praw, em[0:kp, c0:c0 + n], ALU.mult, [B_E], [B_P[ps]])
            else:
                TT("dve", pm.rearrange("p (g n) -> p g n", g=G), praw.rearrange("p (g n) -> p g n", g=G),
                   em[0:kp, c0:c0 + n].unsqueeze(1).broadcast_to([kp, G, n]), ALU.mult, [B_E], [B_P[ps]])
            st["ps"], st["kp"], st["n"] = ps, kp, n

        def att_back(st, h):
            dil, u, n0, half, vs = st["dil"], st["u"], st["n0"], st["half"], st["vs"]
            kp, n, ps = st["kp"], st["n"], st["ps"]
            pm = Pt[ps][0:kp, 512:1024]
            mms = []
            G = len(st["rs"])
            r0 = st["rs"][0]
            pm3 = pm[:, 0:G * n].rearrange("p (g n) -> p g n", g=G)
            for part in range(2):
                t00 = dil * n0 - half * 1024
                j0 = max(0, -(-(part * 512 - t00) // dil))
                j1 = min(n - 1, (part * 512 + 511 - t00) // dil)
                if j1 < j0:
                    continue
                cnt = j1 - j0 + 1
                base0 = t00 + dil * j0 - part * 512
                for k_, r_ in enumerate(st["rs"]):
                    tt0 = base0 + r_
                    rhs = pm[:, k_ * n + j0:k_ * n + j0 + cnt]
                    oap = banks[4 + part][:, tt0:tt0 + dil * (cnt - 1) + 1:dil]
                    mms.append((oap, Vt[vs][0:kp, st["slot_of"][(r_, u)], :], rhs, [B_Vt[vs], B_P[ps]], [PB[4 + part]]))
                if MERGE_DEN and dil > 1:
                    dap = banks[6 + part][:, base0:base0 + dil * cnt].rearrange("p (j d) -> p j d", d=dil)[:, :, r0:r0 + G]
                    drhs = pm3[:, :, j0:j0 + cnt].rearrange("p g j -> p j g")
                    mms.append((dap, onesb[0:kp, :], drhs, [B_ones, B_P[ps]], [PB[6 + part]]))
                else:
                    for k_, r_ in enumerate(st["rs"]):
                        tt0 = base0 + r_
                        rhs = pm[:, k_ * n + j0:k_ * n + j0 + cnt]
                        dap = banks[6 + part][:, tt0:tt0 + dil * (cnt - 1) + 1:dil]
                        mms.append((dap, onesb[0:kp, :], rhs, [B_ones, B_P[ps]], [PB[6 + part]]))
            for i_, (o_, l_, r__, rd_, wr_) in enumerate(mms):
                MM(o_, l_, r__, False, False, rd_, wr_, i_ == len(mms) - 1, skip=True)

        def proj_units(h):
            QT, KT, VT = QKVap[h % 2]
            B_Q, B_K, B_V = QKVb[h % 2]
            wq, wqb = wload(W4, w_in, h * 128, 128)
            wk, wkb = wload(W4, w_in, 1024 + h * 128, 128)
            wv, wvb = wload(W4, w_in, 2048 + h * 128, 128)
            out = []

            def mk(wap, wbuf, dst, dbuf, eng, tg):
                stt = {}

                def piece(bs, q4):
                    if q4 == 0:
                        stt["b"] = psum(bs)
                    bi, bk, bb = stt["b"]
                    for c in range(q4 * 4, q4 * 4 + 4):
                        MM(bk, wap[:, c, 0:128], nT[:, c, tg * 512:(tg + 1) * 512], c == 0, c == KC - 1,
                           [wbuf, B_nT[tg]], [bb], c == KC - 1)
                    if q4 == 3:
                        CP(eng, dst[:, tg * 512:(tg + 1) * 512], bk, [], [bb, dbuf])
                for q4 in range(4):
                    out.append(lambda bs, q4=q4: piece(bs, q4))
            for tg in range(4):
                mk(wk, wkb, KT, B_K, "dve", tg)
            for tg in range(4):
                mk(wv, wvb, VT, B_V, "act", tg)
            for tg in range(4):
                mk(wq, wqb, QT, B_Q, "act", tg)
            return out

        for u_ in proj_units(0):
            u_((0, 1, 2, 3))
        for h in range(NH):
            QT, KT, VT = QKVap[h % 2]
            B_Q, B_K, B_V = QKVb[h % 2]
            filler = proj_units(h + 1) if h + 1 < NH else []
            nback = 0
            vslot = 0
            for half in range(2):
                for b_ in (4, 5, 6, 7):
                    MS("dve", banks[b_][:], 0.0, [PB[b_]])
                pending = []
                for d_i, dil in enumerate((1, 4, 16)):
                    em = Emask[:, d_i * NH + h, :]
                    Lq = 1024 // dil
                    qa, qb = half * Lq, (half + 1) * Lq
                    Lk = 2048 // dil + 64
                    n_own_tiles = 2048 // (128 * dil)
                    vs = vslot % 2
                    vslot += 1
                    need = []
                    for r_ in range(dil):
                        for u in range((Lk + 127) // 128):
                            n0, n1 = max(qa, 128 * u - 64), min(qb, 128 * u + 192)
                            if n1 > n0:
                                need.append((r_, u, n0, n1))
                    slot_of = {}
                    tiles = sorted(set((r_, u) for (r_, u, _, _) in need))
                    for gi in range(0, len(tiles), 8):
                        grp = tiles[gi:gi + 8]
                        bi, bk, bb = psum([1])
                        bkb = bk.bitcast(BF16)
                        for k_, (r_, u) in enumerate(grp):
                            kp = min(128, Lk - 128 * u)
                            if u < n_own_tiles:
                                p0 = dil * 128 * u + r_
                                src = VT[:, p0:p0 + dil * (kp - 1) + 1:dil]
                                sb_ = B_V
                            else:
                                p0 = dil * 128 * u + r_ - 2048
                                src = halo[:, 1, h, p0:p0 + dil * (kp - 1) + 1:dil]
                                sb_ = B_halo[1][h]
                            TR(bkb[0:kp, k_ * 128:(k_ + 1) * 128], src, identb, [sb_, B_idb], [bb], k_ == len(grp) - 1)
                            slot_of[(r_, u)] = gi + k_
                        CP("dve", Vt[vs][:, gi:gi + len(grp), :], bkb[:, 0:len(grp) * 128].rearrange("p (s e) -> p s e", s=len(grp)), [], [bb, B_Vt[vs]])
                    groups = {}
                    for (r_, u, n0, n1) in need:
                        groups.setdefault((u, n0, n1), []).append(r_)
                    for (u, n0, n1), rs in sorted(groups.items()):
                        G = max(1, min(4, 512 // (n1 - n0)))
                        for gi in range(0, len(rs), G):
                            st = dict(u=u, n0=n0, n1=n1, rs=rs[gi:gi + G], dil=dil, half=half, vs=vs, slot_of=slot_of,
                                      em=em, Lk=Lk, n_own=n_own_tiles)
                            att_front(st, h)
                            pending.append(st)
                            if len(pending) > LOOK:
                                att_back(pending.pop(0), h)
                                nback += 1
                                if filler:
                                    filler.pop(0)((0, 1))
                for st in pending:
                    att_back(st, h)
                mslot = (2 * h + half) % 2
                for part in range(2):
                    ACT(rc[:, part * 512:(part + 1) * 512], banks[6 + part][:], AF.Ln, [], [PB[6 + part], B_rc])
                    ACT(rc[:, part * 512:(part + 1) * 512], rc[:, part * 512:(part + 1) * 512], AF.Exp, [], [B_rc], scale=-1.0)
                    TT("dve", mo[mslot][:, part * 512:(part + 1) * 512], banks[4 + part][:], rc[:, part * 512:(part + 1) * 512], ALU.mult, [B_rc], [PB[4 + part], B_mo[mslot]])
                spill(h, mo[mslot], B_mo[mslot], half * 1024, 1024, "mo%d" % mslot)
            for u_ in filler:
                u_((0, 1))
        for r in [r for rr_ in R_QKV for r in rr_] + [R_rc] + R_Vt + R_P + R_mo:
            AR.free(r)
    AR.free(R_E)
    AR.free(R_halo)

    if "skip_ret" not in dbg:
        R_MT = AR.alloc(NH * 512 * 2, "retMT"); MT = R_MT.bf().rearrange("p (h c) -> p h c", h=NH); B_MT = R_MT.buf()
        R_dq = AR.alloc(2 * NH * 128 * 2, "decq"); decq = R_dq.bf().rearrange("p (a h i) -> p a h i", a=2, h=NH); B_dq = R_dq.buf()
        R_cst = AR.alloc(NCST * 4, "cst2"); cst = R_cst.f32(); B_cst = R_cst.buf()
        S.dma("sp", cst, cst_d, "cst2", writes=[B_cst])
        R_tmp = AR.alloc(512 * 4, "setup_tmp2"); tmpf = R_tmp.f32(); B_tmp = R_tmp.buf()
        for h in range(NH):
            TS("dve", tmpf[:, 0:128], cst[:, C_DPOS:C_DPOS + 128], lgL[:, h:h + 1], None, ALU.mult, None, [B_cst, B_sm], [B_tmp])
            STT("dve", tmpf[:, 128:256], cst[:, C_DNEG:C_DNEG + 128], lgR[:, h:h + 1], tmpf[:, 0:128], ALU.mult, ALU.add, [B_cst, B_sm, B_tmp], [B_tmp])
            for rep in range(4):
                ACT(MT[:, h, rep * 128:(rep + 1) * 128], tmpf[:, 128:256], AF.Exp, [B_tmp], [B_MT])
            ACT(decq[:, 0, h, :], cst[:, C_IQ1:C_IQ1 + 128], AF.Exp, [B_cst, B_sm], [B_dq], scale=lgL[:, h:h + 1])
            ACT(decq[:, 1, h, :], cst[:, C_IQR:C_IQR + 128], AF.Exp, [B_cst, B_sm], [B_dq], scale=lgR[:, h:h + 1])

        AR.free(R_tmp)
        AR.free(R_cst)
        def A2(nbytes, name):
            r = AR.alloc(nbytes, name)
            return r, r.buf()
        RQ = [A2(4096, "r_qT%d" % i) for i in range(2)]
        RK = [A2(4096, "r_kT%d" % i) for i in range(2)]
        RV = [A2(4096, "r_vT%d" % i) for i in range(2)]
        RSG = [A2(4096, "r_sgT%d" % i) for i in range(2)]
        R_e, B_e = A2(512 * 4, "r_e"); re_ = R_e.f32()
        R_ql, B_ql = A2(4096, "r_qdL"); qdL = R_ql.bf()
        R_qr, B_qr = A2(4096, "r_qdR"); qdR = R_qr.bf()
        R_kl, B_kl = A2(4096, "r_kdL"); kdLt = R_kl.bf().rearrange("p (c d) -> p c d", c=16)
        R_kr, B_kr = A2(4096, "r_kdR"); kdRt = R_kr.bf().rearrange("p (c d) -> p c d", c=16)
        R_vt, B_vt = A2(4096, "r_vtm"); vtm = R_vt.bf().rearrange("p (c d) -> p c d", c=16)
        R_U, B_U = A2(2 * 16 * 128 * 4, "r_U"); Ut = R_U.f32().rearrange("p (a c e) -> p a c e", a=2, c=16)
        R_Sb, B_Sb = A2(2 * 16 * 128 * 2, "r_Sbf"); Sbf = R_Sb.bf().rearrange("p (a c e) -> p a c e", a=2, c=16)
        R_cur, B_cur = A2(2 * 128 * 4, "r_cur"); cur = R_cur.f32().rearrange("p (a e) -> p a e", a=2)
        B_curs = [B_cur, R_cur.buf()]
        B_Sbs = [B_Sb, R_Sb.buf()]
        R_pt, B_pt = A2(2 * 512 * 2, "r_PT"); PTt = R_pt.bf().rearrange("p (a c) -> p a c", a=2)
        B_ptl = [B_pt, R_pt.buf()]
        R_ss, B_ss = A2(64 * 4, "r_ss"); ss = R_ss.f32()
        R_on, B_on = A2(4096, "r_on"); on = R_on.bf().rearrange("p (c e) -> p c e", c=16)
        R_junk, B_junk = A2(128 * 2, "r_junk"); rjunk = R_junk.bf()
        R_mr, B_mr = A2(4096, "r_mix"); mr = R_mr.bf()

        def ret_A(h):
            s_ = h % 2
            rqT, B_q = RQ[s_][0].bf(), RQ[s_][1]
            rkT, B_k = RK[s_][0].bf(), RK[s_][1]
            rvT, B_v = RV[s_][0].bf(), RV[s_][1]
            rsg, B_sg = RSG[s_][0].bf(), RSG[s_][1]
            wq, wqb = wload(W4, w_in, 3072 + h * 128, 128)
            wk, wkb = wload(W4, w_in, 4096 + h * 128, 128)
            wv, wvb = wload(W4, w_in, 5120 + h * 128, 128)
            wg, wgb = wload(W4, w_in, 6144 + h * 128, 128)
            out = []

            def mk(wap, wbuf, tg, evac):
                stt = {}

                def piece(bs, q4):
                    if q4 == 0:
                        stt["b"] = psum(bs)
                    bi, bk, bb = stt["b"]
                    for c in range(q4 * 4, q4 * 4 + 4):
                        MM(bk, wap[:, c, 0:128], nT[:, c, tg * 512:(tg + 1) * 512], c == 0, c == KC - 1,
                           [wbuf, B_nT[tg]], [bb], c == KC - 1)
                    if q4 == 3:
                        evac(tg, bk, bb)
                for q4 in range(4):
                    out.append(lambda bs, q4=q4: piece(bs, q4))

            def evg(tg, bk, bb):
                ACT(re_, bk, AF.Exp, [], [bb, B_e], scale=-1.0)
                ACT(re_, re_, AF.Ln, [B_eps], [B_e], bias=oneb)
                ACT(re_, re_, AF.Exp, [], [B_e], scale=-1.0)
                TT("dve", rsg[:, tg * 512:(tg + 1) * 512], bk, re_, ALU.mult, [B_e], [bb, B_sg])
            for tg in range(4):
                mk(wk, wkb, tg, lambda tg_, bk, bb: CP("dve", rkT[:, tg_ * 512:(tg_ + 1) * 512], bk, [], [bb, B_k]))
            for tg in range(4):
                mk(wv, wvb, tg, lambda tg_, bk, bb: CP("act", rvT[:, tg_ * 512:(tg_ + 1) * 512], bk, [], [bb, B_v]))
            for tg in range(4):
                mk(wq, wqb, tg, lambda tg_, bk, bb: ACT(rqT[:, tg_ * 512:(tg_ + 1) * 512], bk, AF.Copy, [], [bb, B_q], scale=float(128 ** -0.5)))
            for tg in range(4):
                mk(wg, wgb, tg, evg)
            return out

        fillq = {"q": []}

        def fill(k):
            for _ in range(k):
                if fillq["q"]:
                    fillq["q"].pop(0)((0, 1))

        def ret_B1(h):
            s_ = h % 2
            rqT, B_q = RQ[s_][0].bf(), RQ[s_][1]
            rkT, B_k = RK[s_][0].bf(), RK[s_][1]
            rvT, B_v = RV[s_][0].bf(), RV[s_][1]
            for half in range(2):
                bi, bk, bb = psum([6, 7])
                bkb = bk.bitcast(BF16)
                for c in range(8):
                    cc = half * 8 + c
                    TR(bkb[:, c * 128:(c + 1) * 128], rkT[:, cc * 128:(cc + 1) * 128], identb, [B_k, B_idb], [bb], c == 7)
                src3 = bkb.rearrange("p (c t) -> p c t", c=8)
                TS("dve", kdLt[:, half * 8:half * 8 + 8, :], src3, kdL[:, h:h + 1], None, ALU.mult, None, [B_sm], [bb, B_kl])
                ACT(kdRt[:, half * 8:half * 8 + 8, :], src3, AF.Copy, [B_sm], [bb, B_kr], scale=kdR[:, h:h + 1])
                fill(3)
                bi, bk, bb = psum([6, 7])
                bkb = bk.bitcast(BF16)
                for c in range(8):
                    cc = half * 8 + c
                    TR(bkb[:, c * 128:(c + 1) * 128], rvT[:, cc * 128:(cc + 1) * 128], identb, [B_v, B_idb], [bb], c == 7)
                CP(alt(), vtm[:, half * 8:half * 8 + 8, :], bkb.rearrange("p (c t) -> p c t", c=8), [], [bb, B_vt])
                fill(3)
            for a, (kt_, kb_) in enumerate(((kdLt, B_kl), (kdRt, B_kr))):
                for g4 in range(4):
                    bi, bk, bb = psum([4, 5])
                    for c4 in range(4):
                        c = g4 * 4 + c4
                        MM(bk[:, c4 * 128:(c4 + 1) * 128], kt_[:, c, :], vtm[:, c, :], True, True, [kb_, B_vt], [bb], c4 == 3)
                    CP("act", Ut[:, a, g4 * 4:g4 * 4 + 4, :], bk.rearrange("p (c e) -> p c e", c=4), [], [bb, B_U])
                    fill(1)
            q3 = rqT.rearrange("p (c i) -> p c i", c=16)
            TT("dve", qdL.rearrange("p (c i) -> p c i", c=16), q3, decq[:, 0, h, :].unsqueeze(1).broadcast_to([128, 16, 128]), ALU.mult, [B_q, B_dq], [B_ql])
            TT("dve", qdR.rearrange("p (c i) -> p c i", c=16), q3, decq[:, 1, h, :].unsqueeze(1).broadcast_to([128, 16, 128]), ALU.mult, [B_q, B_dq], [B_qr])
            for i in range(15):
                cL = i + 1
                if cL <= 14:
                    STT("dve", Ut[:, 0, cL, :], Ut[:, 0, cL - 1, :], gCL[:, h:h + 1], Ut[:, 0, cL, :], ALU.mult, ALU.add, [B_sm], [B_U])
                cR = 15 - i
                src_ = SRb[:, h, :] if cR == 15 else Ut[:, 1, cR + 1, :]
                STT("dve", Ut[:, 1, cR, :], src_, gCR[:, h:h + 1], Ut[:, 1, cR, :], ALU.mult, ALU.add, [B_sm, B_srb[h]], [B_U])
            MS("dve", Sbf[:, 0, 0, :], 0.0, [B_Sbs[0]])
            CP("act", Sbf[:, 0, 1:16, :], Ut[:, 0, 0:15, :], [B_U], [B_Sbs[0]])
            CP("dve", Sbf[:, 1, 15, :], SRb[:, h, :], [B_srb[h]], [B_Sbs[1]])
            CP("act", Sbf[:, 1, 0:15, :], Ut[:, 1, 1:16, :], [B_U], [B_Sbs[1]])
            fill(10)

        def ret_B2(h):
            s_ = h % 2
            rqT, B_q = RQ[s_][0].bf(), RQ[s_][1]
            rkT, B_k = RK[s_][0].bf(), RK[s_][1]
            rsg, B_sg = RSG[s_][0].bf(), RSG[s_][1]
            MS("dve", ss[:, 0:16], 0.0, [B_ss])
            for g4 in range(4):
                bi, bk, bb = psum([2, 3])
                for c4 in range(4):
                    c = g4 * 4 + c4
                    MM(bk[:, c4 * 128:(c4 + 1) * 128], rkT[:, c * 128:(c + 1) * 128], rqT[:, c * 128:(c + 1) * 128], True, True, [B_k, B_q], [bb], c4 == 3)
                pa = g4 % 2
                TT("dve", PTt[:, pa, :], bk, MT[:, h, :], ALU.mult, [B_MT], [bb, B_ptl[pa]])
                fill(3)
                oi = [4, 5][g4 % 2]
                ob, obb = banks[oi][:], PB[oi]
                for c4 in range(4):
                    c = g4 * 4 + c4
                    osl = ob[:, c4 * 128:(c4 + 1) * 128]
                    MM(osl, PTt[:, pa, c4 * 128:(c4 + 1) * 128], vtm[:, c, :], True, False, [B_ptl[pa], B_vt], [obb], False)
                    MM(osl, qdL[:, c * 128:(c + 1) * 128], Sbf[:, 0, c, :], False, False, [B_ql, B_Sbs[0]], [obb], False)
                    MM(osl, qdR[:, c * 128:(c + 1) * 128], Sbf[:, 1, c, :], False, True, [B_qr, B_Sbs[1]], [obb], c4 == 3)
                for c4 in range(4):
                    c = g4 * 4 + c4
                    ACT(rjunk, ob[:, c4 * 128:(c4 + 1) * 128], AF.Square, [], [obb, B_junk, B_ss], scale=float(128 ** -0.5), accum=ss[:, c:c + 1])
                ACT(ss[:, 16 + g4 * 4:20 + g4 * 4], ss[:, g4 * 4:g4 * 4 + 4], AF.Ln, [B_eps], [B_ss], bias=epsb)
                ACT(ss[:, 32 + g4 * 4:36 + g4 * 4], ss[:, 16 + g4 * 4:20 + g4 * 4], AF.Exp, [], [B_ss], scale=-0.5)
                for c4 in range(4):
                    c = g4 * 4 + c4
                    ACT(on[:, c, :], ob[:, c4 * 128:(c4 + 1) * 128], AF.Copy, [B_ss], [obb, B_on], scale=ss[:, 32 + c:33 + c])
                fill(3)
            for half in range(2):
                bi, bk, bb = psum([6, 7])
                bkb = bk.bitcast(BF16)
                for c in range(8):
                    cc = half * 8 + c
                    TR(bkb[:, c * 128:(c + 1) * 128], on[:, cc, :], identb, [B_on, B_idb], [bb], c == 7)
                STT("dve", mr[:, half * 1024:(half + 1) * 1024], bkb, pp[:, 32 + h:33 + h], rsg[:, half * 1024:(half + 1) * 1024], ALU.mult, ALU.mult, [B_pp, B_sg], [bb, B_mr])
                fill(2)
            spill(8 + h, mr, B_mr, 0, 2048, "mr")
            fill(100)

        for p_ in ret_A(0):
            p_((0, 1, 2, 3))
        for h in range(NH):
            fillq["q"] = ret_A(h + 1) if h + 1 < NH else []
            ret_B1(h)
            ret_B2(h)
        for r in [x_[0] for x_ in RQ + RK + RV + RSG] + [R_e, R_ql, R_qr, R_kl, R_kr, R_vt, R_U, R_Sb, R_cur, R_pt, R_ss, R_on, R_junk, R_mr]:
            AR.free(r)

    for r in W4["regs"]:
        AR.free(r)
    for r in (R_MT, R_dq, R_nT):
        AR.free(r)
    AR.free(R_srb)
    R_hT = AR.alloc(KC * 1024 * 4, "hT"); hT = R_hT.f32().rearrange("p (c t) -> p c t", c=KC)
    B_hT = [R_hT.buf("hT%d" % tg) for tg in range(2)]
    R_32 = AR.alloc(KC * 1024 * 2, "r32"); r32 = R_32.bf().rearrange("p (c t) -> p c t", c=KC)
    B_32 = [R_32.buf("r32_%d" % tg) for tg in range(2)]
    R_rbc = AR.alloc(1024 * 4, "rbc"); rbc = R_rbc.f32(); B_rbc = [R_rbc.buf("rbc%d" % tg) for tg in range(2)]
    R_xy = [AR.alloc(D * 4, "xy%d" % i) for i in range(2)]
    xy = [r.f32() for r in R_xy]; B_xy = [r.buf() for r in R_xy]
    R_at = [AR.alloc(4 * 1024 * 2, "actT%d" % i) for i in range(2)]
    actT = [r.bf().rearrange("p (j t) -> p j t", j=4) for r in R_at]; B_at = [r.buf() for r in R_at]
    R_sgt = [AR.alloc(512 * 2, "sgt%d" % i) for i in range(2)]
    sgt = [r.bf() for r in R_sgt]; B_sgt = [r.buf() for r in R_sgt]
    W8b = make_wslots(4, 256, "w8b_")
    R_wd = [AR.alloc(4 * D * 2, "wd%d" % i) for i in range(2)]
    wd = [r.bf().rearrange("p (j n) -> p j n", j=4) for r in R_wd]; B_wd = [r.buf() for r in R_wd]
    wd_n = {"n": 0}
    B_y = [Buf("y%d" % i) for i in range(16)]
    mix_v = mix_d.rearrange("h p t -> p h t")
    sgn = {"n": 0}

    def feat_norm(nw_col0, final):
        for tg in range(2):
            if final:
                sqv = R_wd[tg].bf().rearrange("p (c t) -> p c t", c=KC)
                sqb = B_wd[tg]
            else:
                sqv = r32[:, :, tg * 512:(tg + 1) * 512]
                sqb = B_32[tg]
            ACT(sqv, hT[:, :, tg * 512:(tg + 1) * 512], AF.Square, [B_hT[tg]], [sqb])
            bi, bk, bb = psum([6, 7])
            for c in range(KC):
                MM(bk, onesb, sqv[:, c, :], c == 0, c == KC - 1, [B_ones, sqb], [bb], c == KC - 1)
            ACT(rbc[:, tg * 512:(tg + 1) * 512], bk, AF.Ln, [B_eps], [bb, B_rbc[tg]], scale=float(1.0 / D), bias=epsb)
            ACT(rbc[:, tg * 512:(tg + 1) * 512], rbc[:, tg * 512:(tg + 1) * 512], AF.Exp, [], [B_rbc[tg]], scale=-0.5)
        for tg in range(2):
            for c in range(KC):
                if final:
                    STT("dve", hT[:, c, tg * 512:(tg + 1) * 512], hT[:, c, tg * 512:(tg + 1) * 512], pp[:, nw_col0 + c:nw_col0 + c + 1],
                        rbc[:, tg * 512:(tg + 1) * 512], ALU.mult, ALU.mult, [B_pp, B_rbc[tg]], [B_hT[tg]])
                else:
                    STT("dve", r32[:, c, tg * 512:(tg + 1) * 512], hT[:, c, tg * 512:(tg + 1) * 512], pp[:, nw_col0 + c:nw_col0 + c + 1],
                        rbc[:, tg * 512:(tg + 1) * 512], ALU.mult, ALU.mult, [B_pp, B_rbc[tg], B_hT[tg]], [B_32[tg]])

    dbank = {"n": 0}

    def down_group(g, aslot, wslot):
        pairs = [(cb, tg) for cb in range(KC) for tg in range(2)]
        for pi in range(0, len(pairs), 2):
            bsel = []
            for _ in range(2):
                bsel.append([4, 5, 6, 7][dbank["n"] % 4])
                dbank["n"] += 1
            S.prewait("pe", [PB[b_] for b_ in bsel])
            for (cb, tg), b_ in zip(pairs[pi:pi + 2], bsel):
                bk, bb = banks[b_][:], PB[b_]
                for jj in range(4):
                    MM(bk, wd[wslot][:, jj, cb * 128:(cb + 1) * 128], actT[aslot][:, jj, tg * 512:(tg + 1) * 512], jj == 0, jj == 3,
                       [B_wd[wslot], B_at[aslot]], [bb], jj == 3)
                TT("dve", hT[:, cb, tg * 512:(tg + 1) * 512], hT[:, cb, tg * 512:(tg + 1) * 512], bk, ALU.add, [], [bb, B_hT[tg]])

    def prologue_dma(tt):
        tok0 = tt * 1024
        all_mix = [B_mixd[i][tt] for i in range(16)]
        S.dma("sp", r32, mix_v[:, :, tok0:tok0 + 1024], "r32", reads=all_mix, writes=B_32)
        for s in range(2):
            S.dma("sp", xy[s], x[tok0 + s * 128: tok0 + (s + 1) * 128, :], "xy%d" % s, writes=[B_xy[s]])

    prologue_dma(0)
    for tt in range(2):
        tok0 = tt * 1024
        for s in range(8):
            sl = s % 2
            if s >= 2:
                S.dma("sp", xy[sl], x[tok0 + s * 128: tok0 + (s + 1) * 128, :], "xy%d" % sl, writes=[B_xy[sl]])
            for g4 in range(4):
                bi, bk, bb = psum([6, 7])
                for c4 in range(4):
                    cc = g4 * 4 + c4
                    TR(bk[:, c4 * 128:(c4 + 1) * 128], xy[sl][:, cc * 128:(cc + 1) * 128], identf, [B_xy[sl], B_idf], [bb], c4 == 3)
                CP(alt(), hT[:, g4 * 4:g4 * 4 + 4, s * 128:(s + 1) * 128], bk.rearrange("p (c t) -> p c t", c=4), [], [bb, B_hT[s // 4]])
        for u in range(8):
            wap, wbuf = wload(W8b, w_out, u * 256, 256)
            for cb2 in range(2):
                for tg in range(2):
                    bi, bk, bb = psum([0, 1, 2, 3])
                    for c in range(KC):
                        MM(bk, wap[:, c, cb2 * 128:(cb2 + 1) * 128], r32[:, c, tg * 512:(tg + 1) * 512], c == 0, c == KC - 1,
                           [wbuf, B_32[tg]], [bb], c == KC - 1)
                    cb = u * 2 + cb2
                    TT("dve", hT[:, cb, tg * 512:(tg + 1) * 512], hT[:, cb, tg * 512:(tg + 1) * 512], bk, ALU.add, [], [bb, B_hT[tg]])
        if "h1" in dbg and tt == 0:
            pass
        feat_norm(0, False)
        prev = None
        for g in range(11):
            aslot = g % 2
            wslot = wd_n["n"] % 2
            wd_n["n"] += 1
            units = []
            for a in range(2):
                col0 = (g * 4 + a * 2) * 128
                units.append((wload(W8b, w_gate, col0, 256), wload(W8b, w_up, col0, 256)))
            srcd = w_down[g * 512:(g + 1) * 512, :].rearrange("(j p) n -> p j n", p=128)
            S.dma("pool", wd[wslot], srcd, "wd%d" % wslot, writes=[B_wd[wslot]])
            for jj in range(4):
                (wg_ap, wg_b), (wu_ap, wu_b) = units[jj // 2]
                cb2 = jj % 2
                for tg in range(2):
                    gi, gk, gb = psum([0, 1, 2, 3])
                    ui, uk, ub = psum([0, 1, 2, 3])
                    S.prewait("pe", [gb, ub])
                    for c in range(KC):
                        MM(gk, wg_ap[:, c, cb2 * 128:(cb2 + 1) * 128], r32[:, c, tg * 512:(tg + 1) * 512], c == 0, c == KC - 1,
                           [wg_b, B_32[tg]], [gb], c == KC - 1)
                    for c in range(KC):
                        MM(uk, wu_ap[:, c, cb2 * 128:(cb2 + 1) * 128], r32[:, c, tg * 512:(tg + 1) * 512], c == 0, c == KC - 1,
                           [wu_b, B_32[tg]], [ub], c == KC - 1)
                    ss_ = sgn["n"] % 2
                    sgn["n"] += 1
                    ACT(sgt[ss_], gk, AF.Silu, [], [gb, B_sgt[ss_]])
                    TT("dve", actT[aslot][:, jj, tg * 512:(tg + 1) * 512], uk, sgt[ss_], ALU.mult, [B_sgt[ss_]], [ub, B_at[aslot]])
            if prev is not None:
                down_group(*prev)
            prev = (g, aslot, wslot)
        down_group(*prev)
        if tt + 1 < 2:
            prologue_dma(tt + 1)
        feat_norm(16, True)
        for s in range(8):
            sl = s % 2
            yst = R_at[sl].f32()
            for g4 in range(4):
                bi, bk, bb = psum([6, 7])
                for c4 in range(4):
                    cc = g4 * 4 + c4
                    TR(bk[:, c4 * 128:(c4 + 1) * 128], hT[:, cc, s * 128:(s + 1) * 128], identf, [B_hT[s // 4], B_idf], [bb], c4 == 3)
                CP(alt(), yst[:, g4 * 512:(g4 + 1) * 512], bk, [], [bb, B_at[sl]])
            S.dma("sp", y[tok0 + s * 128: tok0 + (s + 1) * 128, :], yst, "yst%d" % sl, reads=[B_at[sl]], writes=[B_y[tt * 8 + s]])
    S.wait_all("sp", B_y)
    if "mixT" in dbg:
        S.wait_all("sp", [b for bb_ in B_mixd for b in bb_])
    print("arena peak (KiB):", AR.peak * 2 / 1024.0, "insts:", S.n_inst)
    S.emit()
    return nc


def make_in_maps(inputs, cores=range(8)):
    f = lambda k: np.ascontiguousarray(np.asarray(inputs[k], dtype=np.float32))
    x = f("x")
    w_in, w_out, w_gate, w_up, w_down = f("w_in")[0], f("w_out")[0], f("w_gate")[0], f("w_up")[0], f("w_down")[0]
    nmix, nffn, nfin = f("norm_mix_w")[0], f("norm_ffn_w")[0], f("norm_final_w")
    retw = f("ret_norm_w")[0]
    dfw, dbw = f("ret_decay_fwd")[0], f("ret_decay_bwd")[0]
    cst = host_consts()
    idxm = np.arange(2048, dtype=np.float32)
    pp = np.concatenate([nffn.reshape(16, 128).T, nfin.reshape(16, 128).T, retw.reshape(8, 128).T], axis=1)
    pp = np.ascontiguousarray(pp, dtype=np.float32)
    maps = []
    for c in cores:
        b, hf = c // 2, c % 2
        xb = x[b] if hf == 0 else x[b][::-1]
        dL, dR = (dfw, dbw) if hf == 0 else (dbw, dfw)
        bcv = np.concatenate([nmix, dL, dR]).astype(np.float32)
        maps.append({"x": np.ascontiguousarray(xb), "w_in": w_in, "w_out": w_out, "w_gate": w_gate, "w_up": w_up,
                     "w_down": w_down, "cst": cst, "pp": pp, "bcv": bcv, "idxm": idxm})
    return maps


def kernel(**inputs):
    nc = build_program()
    maps = make_in_maps(inputs)
    res = run_bass_kernel_spmd(nc, maps, core_ids=list(range(8)))
    out = np.empty((4, 4096, 2048), np.float32)
    for c in range(8):
        b, hf = c // 2, c % 2
        yc = np.asarray(res.results[c]["y"], dtype=np.float32)
        if hf == 0:
            out[b, :2048] = yc
        else:
            out[b, 2048:] = yc[::-1]
    return out
```
